# Optimizing a Trainium2 kernel written in Bass

```python
import math
import jax
import jax.numpy as jnp
from jax import lax
import numpy as np

D_MODEL = 1024
BATCH = 32
SEQ = 256
DEPTH = 2
DEC_BATCH = 8
DEC_SEQ = 2048
PAST_LEN = 256

GRID_W = 64
EPS = 1e-6
MLA_HEADS = 8
Q_LORA = 384
KV_LORA = 256
QK_NOPE = 64
QK_ROPE = 32
V_HEAD = 64
MLA_W = MLA_HEADS * V_HEAD
ROPE_THETA = 10000.0
Q_BLOCK = 128
CONV_W = 512
CONV_K = 3
GDN_HEADS = 8
GDN_DK = 64
GDN_DV = 64
GDN_QK_W = GDN_HEADS * GDN_DK
GDN_V_W = GDN_HEADS * GDN_DV
GDN_CHUNK = 64
N_BRANCH = 3
IN_SIZES = (Q_LORA, KV_LORA, QK_ROPE, MLA_W,
            CONV_W, CONV_W, CONV_W, CONV_W,
            GDN_QK_W, GDN_QK_W, GDN_V_W, GDN_V_W,
            GDN_HEADS, GDN_HEADS, GDN_HEADS, GDN_HEADS,
            N_BRANCH * D_MODEL)
D_IN = sum(IN_SIZES)

kernel_name = 'hybrid_mla_conv_gdn_diffusion_step'


def rms_norm(x, g):
    xf = x.astype(jnp.float32)
    y = xf * lax.rsqrt(jnp.mean(xf * xf, axis=-1, keepdims=True) + EPS)
    return (y * g.astype(jnp.float32)).astype(x.dtype)


def l2_normalize(x):
    return x * lax.rsqrt(jnp.sum(x * x, axis=-1, keepdims=True) + EPS)


def split_columns(u):
    parts, start = [], 0
    for n in IN_SIZES:
        parts.append(u[..., start:start + n])
        start += n
    return parts


def axial_rope_tables(n_tokens):
    rows = n_tokens // GRID_W
    t = jnp.arange(rows * GRID_W)
    row = (t // GRID_W).astype(jnp.float32)
    col = (t % GRID_W).astype(jnp.float32)
    n_freq = QK_ROPE // 4
    inv_freq = ROPE_THETA ** (-jnp.arange(n_freq, dtype=jnp.float32) / n_freq)
    ang = jnp.concatenate([row[:, None] * inv_freq, col[:, None] * inv_freq], axis=-1)
    return jnp.cos(ang), jnp.sin(ang)


def apply_rope(x, cos, sin):
    xf = x.astype(jnp.float32).reshape(x.shape[:-1] + (QK_ROPE // 2, 2))
    x0, x1 = xf[..., 0], xf[..., 1]
    out = jnp.stack([x0 * cos - x1 * sin, x0 * sin + x1 * cos], axis=-1)
    return out.reshape(x.shape).astype(x.dtype)


def short_conv(x, w):
    s = x.shape[1]
    pad = CONV_K // 2
    xp = jnp.pad(x, ((0, 0), (pad, pad), (0, 0)))
    y = xp[:, 0:s] * w[0]
    for i in range(1, CONV_K):
        y = y + xp[:, i:i + s] * w[i]
    return y


def adaln_projection(x, cond, p):
    mod = jax.nn.silu(cond) @ p['w_ada'] + p['b_ada']
    shift, scale, gate = jnp.split(mod[:, None, :], 3, axis=-1)
    h = rms_norm(x, p['norm_g']) * (1.0 + scale) + shift
    return split_columns(h @ p['w_in']), gate


def mla_queries_and_latent(cq, ckv, p):
    b, s, _ = cq.shape
    q = (rms_norm(cq, p['q_norm_g']) @ p['w_uq']).reshape(b, s, MLA_HEADS, QK_NOPE + QK_ROPE)
    return q[..., :QK_NOPE], q[..., QK_NOPE:], rms_norm(ckv, p['kv_norm_g'])


def mla_expand(ckv_n, w_ukv):
    b, s, _ = ckv_n.shape
    kv = (ckv_n @ w_ukv).reshape(b, s, MLA_HEADS, QK_NOPE + V_HEAD)
    return kv[..., :QK_NOPE], kv[..., QK_NOPE:]


def mla_attention(q_nope, q_rope, k_nope, k_rope, v):
    b, sq, h, _ = q_nope.shape
    nb = sq // Q_BLOCK
    scale = (QK_NOPE + QK_ROPE) ** -0.5

    def to_blocks(a):
        return jnp.moveaxis(a.reshape((b, nb, Q_BLOCK) + a.shape[2:]), 1, 0)

    def block(qs):
        qn, qr = qs
        s = (jnp.einsum('bqhd,bkhd->bhqk', qn, k_nope)
             + jnp.einsum('bqhr,bkr->bhqk', qr, k_rope)).astype(jnp.float32) * scale
        pr = jax.nn.softmax(s, axis=-1).astype(v.dtype)
        return jnp.einsum('bhqk,bkhd->bqhd', pr, v)

    o = lax.map(block, (to_blocks(q_nope), to_blocks(q_rope)))
    return jnp.moveaxis(o, 0, 1).reshape(b, sq, h * V_HEAD)


def conv_branch(b_in, c_in, x_in, gate, w):
    return b_in * short_conv(c_in * x_in, w) * jax.nn.silu(gate)


def gated_delta_chunked(q, k, v, g, beta, h0):
    b, s, h, dk = q.shape
    dv = v.shape[-1]
    c = GDN_CHUNK
    n = s // c

    def chunks(a):
        a = a.reshape((b, n, c, h) + a.shape[3:])
        return jnp.moveaxis(jnp.moveaxis(a, 1, 0), 3, 2)

    qc, kc, vc, bc = chunks(q), chunks(k), chunks(v), chunks(beta)
    gc = jnp.cumsum(chunks(g), axis=-1)
    idx = jnp.arange(c)
    lower = idx[:, None] >= idx[None, :]
    strict = idx[:, None] > idx[None, :]
    decay = jnp.exp(jnp.where(lower, gc[..., :, None] - gc[..., None, :], -jnp.inf))
    kb = kc * bc[..., None]
    a_mat = jnp.where(strict, jnp.einsum('nbhik,nbhjk->nbhij', kb, kc) * decay, 0.0) + jnp.eye(c, dtype=jnp.float32)
    rhs = jnp.concatenate([vc * bc[..., None], kb * jnp.exp(gc)[..., None]], axis=-1)
    sol = lax.linalg.triangular_solve(a_mat, rhs, left_side=True, lower=True, unit_diagonal=True)
    u, w = sol[..., :dv], sol[..., dv:]
    attn = jnp.where(lower, jnp.einsum('nbhik,nbhjk->nbhij', qc, kc) * decay, 0.0)
    q_dec = qc * jnp.exp(gc)[..., None]
    g_last = gc[..., -1]
    k_dec = kc * jnp.exp(g_last[..., None] - gc)[..., None]

    def step(state, xs):
        u_i, w_i, qd_i, kd_i, at_i, gl_i = xs
        v_new = u_i - jnp.einsum('bhck,bhkv->bhcv', w_i, state)
        o_i = jnp.einsum('bhck,bhkv->bhcv', qd_i, state) + jnp.einsum('bhij,bhjv->bhiv', at_i, v_new)
        state = state * jnp.exp(gl_i)[..., None, None] + jnp.einsum('bhck,bhcv->bhkv', kd_i, v_new)
        return state, o_i

    h_fin, o = lax.scan(step, h0, (u, w, q_dec, k_dec, attn, g_last))
    o = jnp.swapaxes(jnp.moveaxis(o, 0, 1), 2, 3).reshape(b, s, h, dv)
    return o, h_fin


def gdn_branch(q_in, k_in, v_in, z, a_f, a_b, b_f, b_b, h0, p):
    b, s, _ = q_in.shape
    f32 = jnp.float32
    qkv = jax.nn.silu(short_conv(jnp.concatenate([q_in, k_in, v_in], axis=-1), p['conv_qkv_w'])).astype(f32)
    q = l2_normalize(qkv[..., :GDN_QK_W].reshape(b, s, GDN_HEADS, GDN_DK)) * (GDN_DK ** -0.5)
    k = l2_normalize(qkv[..., GDN_QK_W:2 * GDN_QK_W].reshape(b, s, GDN_HEADS, GDN_DK))
    v = qkv[..., 2 * GDN_QK_W:].reshape(b, s, GDN_HEADS, GDN_DV)
    a_log = p['a_log'].astype(f32)
    dt_bias = p['dt_bias'].astype(f32)
    g_f = -jnp.exp(a_log[0]) * jax.nn.softplus(a_f.astype(f32) + dt_bias[0])
    g_b = -jnp.exp(a_log[1]) * jax.nn.softplus(a_b.astype(f32) + dt_bias[1])
    beta_f = jax.nn.sigmoid(b_f.astype(f32))
    beta_b = jax.nn.sigmoid(b_b.astype(f32))
    h0 = h0.astype(f32)
    o_f, h_f = gated_delta_chunked(q, k, v, g_f, beta_f, h0[:, 0])
    rev = lambda a: jnp.flip(a, axis=1)
    o_r, h_r = gated_delta_chunked(rev(q), rev(k), rev(v), rev(g_b), rev(beta_b), h0[:, 1])
    o = rms_norm(o_f + rev(o_r), p['gdn_norm_g']).reshape(b, s, GDN_V_W).astype(z.dtype) * jax.nn.silu(z)
    return o, jnp.stack([h_f, h_r], axis=1)


def merge_branches(o_a, o_b, o_c, merge_g, p):
    s_a, s_b, s_c = jnp.split(jax.nn.sigmoid(merge_g), N_BRANCH, axis=-1)
    m = s_a * (o_a @ p['w_pa']) + s_b * (o_b @ p['w_pb']) + s_c * (o_c @ p['w_pc'])
    return m @ p['w_o']


def context_layer(x, cond, p):
    b, s, _ = x.shape
    parts, res_gate = adaln_projection(x, cond, p)
    cq, ckv, kpe, gate_a, b_in, c_in, x_in, gate_b, q_in, k_in, v_in, z, a_f, a_b, b_f, b_b, merge_g = parts
    q_nope, q_rope, ckv_n = mla_queries_and_latent(cq, ckv, p)
    k_nope, v = mla_expand(ckv_n, p['w_ukv'])
    o_a = mla_attention(q_nope, q_rope, k_nope, kpe, v) * jax.nn.silu(gate_a)
    o_b = conv_branch(b_in, c_in, x_in, gate_b, p['conv_b_w'])
    h0 = jnp.zeros((b, 2, GDN_HEADS, GDN_DK, GDN_DV), jnp.float32)
    o_c, h_fin = gdn_branch(q_in, k_in, v_in, z, a_f, a_b, b_f, b_b, h0, p)
    x = x + res_gate * merge_branches(o_a, o_b, o_c, merge_g, p)
    return x, ckv_n, kpe, h_fin.astype(x.dtype)


def latent_layer(x, cond, ckv_ctx, kpe_ctx, h_ctx, cos, sin, p):
    parts, res_gate = adaln_projection(x, cond, p)
    cq, ckv, kpe, gate_a, b_in, c_in, x_in, gate_b, q_in, k_in, v_in, z, a_f, a_b, b_f, b_b, merge_g = parts
    q_nope, q_rope, ckv_n = mla_queries_and_latent(cq, ckv, p)
    q_rope = apply_rope(q_rope, cos[:, None, :], sin[:, None, :])
    kpe = apply_rope(kpe, cos, sin)
    k_nope_l, v_l = mla_expand(ckv_n, p['w_ukv'])
    k_nope_c, v_c = mla_expand(ckv_ctx.astype(x.dtype), p['w_ukv'])
    k_nope = jnp.concatenate([k_nope_c, k_nope_l], axis=1)
    k_rope = jnp.concatenate([kpe_ctx.astype(x.dtype), kpe], axis=1)
    v = jnp.concatenate([v_c, v_l], axis=1)
    o_a = mla_attention(q_nope, q_rope, k_nope, k_rope, v) * jax.nn.silu(gate_a)
    o_b = conv_branch(b_in, c_in, x_in, gate_b, p['conv_b_w'])
    o_c, _ = gdn_branch(q_in, k_in, v_in, z, a_f, a_b, b_f, b_b, h_ctx, p)
    return x + res_gate * merge_branches(o_a, o_b, o_c, merge_g, p)


def setup_inputs(seed: int = 0) -> dict:
    key = jax.random.key(seed)
    ks = jax.random.split(key, 32)
    f32 = jnp.float32
    D = D_MODEL

    def nrm(k, shape, scale):
        return jax.random.normal(k, shape, f32) * scale

    a_vals = jax.random.uniform(ks[17], (DEPTH, 2, GDN_HEADS), f32, 1.0, 16.0)
    dt = jnp.exp(jax.random.uniform(ks[18], (DEPTH, 2, GDN_HEADS), f32, math.log(1e-3), math.log(1e-1)))
    return {
        'x_prompt': nrm(ks[0], (BATCH, SEQ, D), 1.0),
        'x_sample': nrm(ks[1], (DEC_BATCH, DEC_SEQ, D), 1.0),
        'c': nrm(ks[2], (DEC_BATCH, D), 1.0),
        'cache_ckv': nrm(ks[3], (DEC_BATCH, DEPTH, PAST_LEN, KV_LORA), 1.0),
        'cache_kpe': nrm(ks[4], (DEC_BATCH, DEPTH, PAST_LEN, QK_ROPE), 1.0),
        'state_gdn': nrm(ks[5], (DEC_BATCH, DEPTH, 2, GDN_HEADS, GDN_DK, GDN_DV), 0.1),
        'c_ctx': nrm(ks[6], (D,), 1.0),
        'norm_g': 1.0 + nrm(ks[7], (DEPTH, D), 0.02),
        'w_ada': nrm(ks[8], (DEPTH, D, 3 * D), 0.5 * D ** -0.5),
        'b_ada': nrm(ks[9], (DEPTH, 3 * D), 0.02),
        'w_in': nrm(ks[10], (DEPTH, D, D_IN), D ** -0.5),
        'q_norm_g': 1.0 + nrm(ks[11], (DEPTH, Q_LORA), 0.02),
        'kv_norm_g': 1.0 + nrm(ks[12], (DEPTH, KV_LORA), 0.02),
        'w_uq': nrm(ks[13], (DEPTH, Q_LORA, MLA_HEADS * (QK_NOPE + QK_ROPE)), Q_LORA ** -0.5),
        'w_ukv': nrm(ks[14], (DEPTH, KV_LORA, MLA_HEADS * (QK_NOPE + V_HEAD)), KV_LORA ** -0.5),
        'conv_b_w': nrm(ks[15], (DEPTH, CONV_K, CONV_W), CONV_K ** -0.5),
        'conv_qkv_w': nrm(ks[16], (DEPTH, CONV_K, 2 * GDN_QK_W + GDN_V_W), CONV_K ** -0.5),
        'a_log': jnp.log(a_vals),
        'dt_bias': dt + jnp.log(-jnp.expm1(-dt)),
        'gdn_norm_g': 1.0 + nrm(ks[19], (DEPTH, GDN_DV), 0.02),
        'w_pa': nrm(ks[20], (DEPTH, MLA_W, D), MLA_W ** -0.5),
        'w_pb': nrm(ks[21], (DEPTH, CONV_W, D), CONV_W ** -0.5),
        'w_pc': nrm(ks[22], (DEPTH, GDN_V_W, D), GDN_V_W ** -0.5),
        'w_o': nrm(ks[23], (DEPTH, D, D), D ** -0.5),
        'final_norm_g': 1.0 + nrm(ks[24], (D,), 0.02),
    }


def reference(x_prompt, x_sample, c, cache_ckv, cache_kpe, state_gdn, c_ctx, norm_g, w_ada, b_ada,
              w_in, q_norm_g, kv_norm_g, w_uq, w_ukv, conv_b_w, conv_qkv_w, a_log, dt_bias,
              gdn_norm_g, w_pa, w_pb, w_pc, w_o, final_norm_g):
    cond_ctx = jnp.broadcast_to(c_ctx[None, :], (x_prompt.shape[0], D_MODEL))
    cos, sin = axial_rope_tables(x_sample.shape[1])
    xp, xs = x_prompt, x_sample
    ckv_out, kpe_out, st_out = [], [], []
    for l in range(DEPTH):
        p = {'norm_g': norm_g[l], 'w_ada': w_ada[l], 'b_ada': b_ada[l], 'w_in': w_in[l],
             'q_norm_g': q_norm_g[l], 'kv_norm_g': kv_norm_g[l], 'w_uq': w_uq[l], 'w_ukv': w_ukv[l],
             'conv_b_w': conv_b_w[l], 'conv_qkv_w': conv_qkv_w[l], 'a_log': a_log[l],
             'dt_bias': dt_bias[l], 'gdn_norm_g': gdn_norm_g[l], 'w_pa': w_pa[l], 'w_pb': w_pb[l],
             'w_pc': w_pc[l], 'w_o': w_o[l]}
        xp, ckv_n, kpe, h_fin = context_layer(xp, cond_ctx, p)
        ckv_out.append(ckv_n)
        kpe_out.append(kpe)
        st_out.append(h_fin)
        xs = latent_layer(xs, c, cache_ckv[:, l], cache_kpe[:, l], state_gdn[:, l], cos, sin, p)
    y_prompt = rms_norm(xp, final_norm_g)
    y_sample = rms_norm(xs, final_norm_g)
    new_cache_ckv = jnp.stack(ckv_out, axis=1)
    new_cache_kpe = jnp.stack(kpe_out, axis=1)
    new_state_gdn = jnp.stack(st_out, axis=1)
    return (y_prompt, y_sample, new_cache_ckv, new_cache_kpe, new_state_gdn)
```

```python
import numpy as np
import ml_dtypes
from contextlib import ExitStack
import concourse.bass as bass
import concourse.mybir as mybir
from concourse.bass_utils import run_bass_kernel_spmd

F32 = mybir.dt.float32
BF16 = mybir.dt.bfloat16
AF = mybir.ActivationFunctionType
ALU = mybir.AluOpType
AX = mybir.AxisListType

D = 1024
DEPTH = 2
EPS = 1e-6
DIN = 8384
C_CQ, C_CKV, C_KPE, C_GA = 0, 384, 640, 672
C_B, C_C, C_X, C_GB = 1184, 1696, 2208, 2720
C_Q, C_K, C_V, C_Z = 3232, 3744, 4256, 4768
C_AB = 5280
C_MG = 5312
NEG = -30000.0


class Tok:
    __slots__ = ("name", "w", "r", "excl")

    def __init__(self, name="", excl=False):
        self.name = name
        self.w = None
        self.r = []
        self.excl = excl


class _Eng:
    def __init__(self, name, eng, sem):
        self.name = name
        self.eng = eng
        self.sem = sem
        self.count = 0
        self.waited = {}


class Ctx:
    def __init__(self, nc, stack, n_dma_sems=40, same_engine_sync=True):
        self.nc = nc
        self.same_engine_sync = same_engine_sync
        self.sems = {}
        self.engs = {}
        for name, eng in (("pe", nc.tensor), ("act", nc.scalar), ("dve", nc.vector),
                          ("pool", nc.gpsimd), ("sp", nc.sync)):
            self.sems["s_" + name] = stack.enter_context(nc.semaphore("s_" + name))
            self.engs[name] = _Eng(name, eng, "s_" + name)
        self.dma_sems = []
        self.dma_pools = {"sp": [], "pool": []}
        for i in range(n_dma_sems):
            k = "d_%d" % i
            self.sems[k] = stack.enter_context(nc.semaphore(k))
            slot = [k, 0]
            self.dma_sems.append(slot)
            self.dma_pools["pool" if i >= n_dma_sems - 12 else "sp"].append(slot)
        self.dma_rr = {"sp": 0, "pool": 0}
        self.n_instr = 0
        self.n_wait = 0
        self.stopped = False

    def _wait(self, e, semkey, val):
        if e.waited.get(semkey, 0) >= val:
            return
        if semkey == e.sem:
            if e.name == "pe" or not self.same_engine_sync:
                return
        e.eng.wait_ge(self.sems[semkey], val)
        e.waited[semkey] = val
        self.n_wait += 1

    def _deps(self, e, reads, writes):
        for t in reads:
            if t.w is not None:
                self._wait(e, *t.w)
            if t.excl:
                for ev in t.r:
                    if ev[0] != e.sem:
                        self._wait(e, *ev)
        for t in writes:
            if t.w is not None:
                self._wait(e, *t.w)
            for ev in t.r:
                self._wait(e, *ev)

    def _mark(self, ev, reads, writes):
        for t in reads:
            t.r = [x for x in t.r if x[0] != ev[0]]
            t.r.append(ev)
        for t in writes:
            t.w = ev
            t.r = []

    def op(self, engname, fn, reads=(), writes=()):
        if self.stopped:
            return None
        e = self.engs[engname]
        self._deps(e, reads, writes)
        ins = fn(e.eng)
        e.count += 1
        ins.then_inc(self.sems[e.sem], 1)
        self._mark((e.sem, e.count), reads, writes)
        self.n_instr += 1
        return ins

    def mm(self, fns, reads=(), writes=()):
        if self.stopped:
            return None
        e = self.engs["pe"]
        self._deps(e, reads, writes)
        ins = None
        for fn in fns:
            ins = fn(e.eng)
            self.n_instr += 1
        e.count += 1
        ins.then_inc(self.sems[e.sem], 1)
        self._mark((e.sem, e.count), reads, writes)
        return ins

    def dma(self, out, in_, reads=(), writes=(), q="sp", **kw):
        if self.stopped:
            return None
        e = self.engs[q]
        self._deps(e, reads, writes)
        pool_ = self.dma_pools["pool" if q == "pool" else "sp"]
        slot = pool_[self.dma_rr["pool" if q == "pool" else "sp"] % len(pool_)]
        self.dma_rr["pool" if q == "pool" else "sp"] += 1
        semkey, cnt = slot
        if cnt > 0:
            self._wait(e, semkey, 16 * cnt)
        ins = e.eng.dma_start(out=out, in_=in_, **kw)
        slot[1] = cnt + 1
        ins.then_inc(self.sems[semkey], 16)
        self._mark((semkey, 16 * (cnt + 1)), reads, writes)
        self.n_instr += 1

    def wait_all(self, engname, toks):
        if self.stopped and engname != "sp":
            return
        e = self.engs[engname]
        for t in toks:
            if t.w is not None:
                self._wait(e, *t.w)
            for ev in t.r:
                self._wait(e, *ev)


class Buf:
    def __init__(self, t, name):
        self.t = t
        self.k = Tok(name)

    def __getitem__(self, idx):
        return self.t[idx]


class QBuf(Buf):
    def __init__(self, t, name):
        Buf.__init__(self, t, name)
        self.kq = [Tok("%s_q%d" % (name, i)) for i in range(4)]
        self.k = None

    @property
    def ka(self):
        return list(self.kq)


class StopBuild(Exception):
    pass


class Builder:
    def __init__(self, NP=4, SP=256, SS=2048, PAST=256, dbg=None):
        self.NP, self.SP, self.SS, self.PAST = NP, SP, SS, PAST
        self.TP = NP * SP
        self.TT = self.TP + SS
        self.dbg = dbg or {}
        self.nc = bass.Bass("TRN2", target_bir_lowering=False)
        self.cast_rr = 0
        self.ev_rr = 0
        self.store_q = "pool"
        self._dumped = set()

    def dram_in(self, name, shape):
        return self.nc.dram_tensor(name, list(shape), F32, kind="ExternalInput").ap()

    def dram_out(self, name, shape):
        return self.nc.dram_tensor(name, list(shape), F32, kind="ExternalOutput").ap()

    def sb(self, name, shape, dt=F32, quarters=False):
        self.uid = getattr(self, "uid", 0) + 1
        name = "%s_u%d" % (name, self.uid)
        t = self.scope.enter_context(self.nc.sbuf_tensor(name, list(shape), dt))
        return QBuf(t, name) if quarters else Buf(t, name)

    def declare(self):
        NP, SP, SS, PAST = self.NP, self.SP, self.SS, self.PAST
        i = {}
        i["xp"] = self.dram_in("xp", [self.TP, D])
        i["xs"] = self.dram_in("xs", [SS, D])
        i["cvec"] = self.dram_in("cvec", [2, D])
        i["cckv"] = self.dram_in("cckv", [DEPTH, PAST, 256])
        i["ckpe"] = self.dram_in("ckpe", [DEPTH, PAST, 32])
        i["sgdn"] = self.dram_in("sgdn", [DEPTH, 2, 8, 64, 64])
        i["norm_g"] = self.dram_in("norm_g", [DEPTH, D])
        i["w_ada"] = self.dram_in("w_ada", [DEPTH, D, 3 * D])
        i["b_ada"] = self.dram_in("b_ada", [DEPTH, 3 * D])
        i["w_in"] = self.dram_in("w_in", [DEPTH, D, DIN])
        i["q_norm_g"] = self.dram_in("q_norm_g", [DEPTH, 384])
        i["kv_norm_g"] = self.dram_in("kv_norm_g", [DEPTH, 256])
        i["w_uq"] = self.dram_in("w_uq", [DEPTH, 384, 768])
        i["w_ukv"] = self.dram_in("w_ukv", [DEPTH, 256, 1024])
        i["conv_b_w"] = self.dram_in("conv_b_w", [DEPTH, 3, 512])
        i["conv_qkv_w"] = self.dram_in("conv_qkv_w", [DEPTH, 3, 1536])
        i["a_log"] = self.dram_in("a_log", [DEPTH, 16])
        i["dt_bias"] = self.dram_in("dt_bias", [DEPTH, 16])
        i["gdn_norm_g"] = self.dram_in("gdn_norm_g", [DEPTH, 64])
        i["w_pa"] = self.dram_in("w_pa", [DEPTH, 512, D])
        i["w_pb"] = self.dram_in("w_pb", [DEPTH, 512, D])
        i["w_pc"] = self.dram_in("w_pc", [DEPTH, 512, D])
        i["w_o"] = self.dram_in("w_o", [DEPTH, D, D])
        i["final_norm_g"] = self.dram_in("final_norm_g", [D])
        i["k_ident"] = self.dram_in("k_ident", [128, 128])
        i["k_rope"] = self.dram_in("k_rope", [2, 96, SS])
        i["k_mask"] = self.dram_in("k_mask", [8, 128, 128])
        for k, shp in self.dbg.get("in", {}).items():
            i[k] = self.dram_in(k, shp)
        self.i = i
        o = {}
        o["yp"] = self.dram_out("yp", [self.TP, D])
        o["ys"] = self.dram_out("ys", [SS, D])
        o["nckv"] = self.dram_out("nckv", [NP, DEPTH, SP, 256])
        o["nkpe"] = self.dram_out("nkpe", [NP, DEPTH, SP, 32])
        o["nst"] = self.dram_out("nst", [NP, DEPTH, 2, 8, 64, 64])
        for k, shp in self.dbg.get("out", {}).items():
            o[k] = self.dram_out(k, shp)
        self.o = o
        self.xres = self.nc.dram_tensor("xres", [self.TT, D], F32, kind="Internal").ap()
        self.oTs = self.nc.dram_tensor("oTs", [12, 128, self.TT], BF16, kind="Internal").ap()
        self.k_oTs = {}
        self.ofs = self.nc.dram_tensor("ofs", [2, self.TT, 512], BF16, kind="Internal").ap()
        self.k_ofs = {}
        self.k_xres = {r0: Tok("xres%d" % r0) for r0 in range(0, self.TT, 128)}
        self.out_toks = []

    def cast_eng(self):
        self.cast_rr += 1
        return ("act", "dve")[self.cast_rr % 2]

    def copy(self, engname, out, in_, reads, writes, scale=None):
        if engname == "act":
            if scale is None:
                self.cx.op("act", lambda e: e.activation(out=out, in_=in_, func=AF.Copy), reads, writes)
            else:
                self.cx.op("act", lambda e: e.activation(out=out, in_=in_, func=AF.Copy, scale=scale), reads, writes)
        else:
            if scale is None:
                self.cx.op(engname, lambda e: e.tensor_copy(out=out, in_=in_), reads, writes)
            else:
                self.cx.op(engname, lambda e: e.tensor_scalar(out=out, in0=in_, scalar1=scale, scalar2=None,
                                                               op0=ALU.mult), reads, writes)

    def load_w(self, src, dst, dst_tok, K, N, row_scale=None, extra_reads=()):
        KC = K // 128
        per = max(1, self.STG // KC)
        n0 = 0
        while n0 < N:
            n = min(per, N - n0)
            s = self.stg[self.stg_rr % len(self.stg)]
            self.stg_rr += 1
            sv = s.t[:, 0:KC * n].rearrange("p (k n) -> p k n", k=KC)
            self.cx.dma(sv, src[:, n0:n0 + n].rearrange("(k p) n -> p k n", p=128), writes=[s.k])
            if row_scale is None:
                self.copy(self.cast_eng(), dst[:, :, n0:n0 + n], sv, [s.k] + list(extra_reads), [dst_tok])
            else:
                for kc in range(KC):
                    self.copy(("act", "dve")[kc % 2], dst[:, kc, n0:n0 + n], sv[:, kc, :],
                              [s.k] + list(extra_reads), [dst_tok], scale=row_scale[:, kc:kc + 1])
            n0 += n

    def stop_at(self, name):
        if self.dbg.get("stop") == name:
            self.cx.stopped = True

    def bank(self, pool=None):
        if pool is None:
            b = self.ps[self.ps_rr % len(self.ps)]
            self.ps_rr += 1
            return b
        lst = self.pools[pool]
        c = self.pool_rr.get(pool, 0)
        self.pool_rr[pool] = c + 1
        return self.ps[lst[c % len(lst)]]

    def set_pools(self, pools):
        self.pools = pools
        self.pool_rr = {}

    def rsqrt_mean(self, ss, out, n, reads, writes, tmp):
        self.cx.op("act", lambda e: e.activation(out=tmp, in_=ss, func=AF.Ln, scale=1.0 / n, bias=EPS), reads, writes)
        self.cx.op("act", lambda e: e.activation(out=out, in_=tmp, func=AF.Exp, scale=-0.5), writes, writes)

    def build(self):
        nc = self.nc
        self.declare()
        with ExitStack() as top:
            self.scope = top
            self.cx = Ctx(nc, top)
            cx = self.cx
            self.ps = [Buf(top.enter_context(nc.psum_tensor("ps%d" % b, [128, 512], F32)), "ps%d" % b)
                       for b in range(8)]
            for b in self.ps:
                b.k.excl = True
            self.ps_rr = 0
            self.STG = 1024
            self.stg = [self.sb("stg%d" % b, [128, self.STG]) for b in range(4)]
            self.stg_rr = 0
            self.ident = self.sb("ident", [128, 128], BF16)
            self.identf = self.sb("identf", [128, 128])
            cx.dma(self.identf[:], self.i["k_ident"], writes=[self.identf.k])
            self.copy("dve", self.ident[:], self.identf[:], [self.identf.k], [self.ident.k])
            self.ones_f = self.sb("ones_f", [128, 128])
            cx.op("pool", lambda e: e.memset(self.ones_f[:], 1.0), (), [self.ones_f.k])
            self.csil = self.sb("csil", [128, 2, 8], BF16)
            self.cload = self.sb("cload", [128, 2, 8])
            cx.dma(self.cload[:], self.i["cvec"].rearrange("c (k p) -> p c k", p=128), writes=[self.cload.k],
                   allow_slow_non_contiguous=True)
            cx.op("act", lambda e: e.activation(out=self.csil[:], in_=self.cload[:], func=AF.Silu),
                  [self.cload.k], [self.csil.k])
            self.modcol = self.sb("modcol", [128, 2, 24])
            self.gate_bc = self.sb("gate_bc", [128, 2, D])
            groups = [(0, self.TP, 0, "p"), (self.TP, self.SS, 1, "s")]
            for l in range(DEPTH):
                self.phase0(l)
                for (t0, T, cond, kind) in groups:
                    with ExitStack() as gs:
                        self.scope = gs
                        self.group(l, t0, T, cond, kind)
                    self.scope = top
            cx.wait_all("sp", self.out_toks)
            sp = cx.engs["sp"]
            for nm, ee in cx.engs.items():
                if ee.count and nm != "sp":
                    sp.eng.wait_ge(cx.sems[ee.sem], ee.count)
            for semkey, cnt in cx.dma_sems:
                if cnt:
                    sp.eng.wait_ge(cx.sems[semkey], 16 * cnt)
        return nc

    def phase0(self, l):
        cx, nc = self.cx, self.nc
        with ExitStack() as sc:
            old = self.scope
            self.scope = sc
            wbf2 = [self.sb("wada_bf%d" % b, [128, 8, 512], BF16) for b in range(2)]
            modrow = self.sb("modrow", [1, 2, 3 * D])
            brow = self.sb("brow", [1, 3 * D])
            ngc = self.sb("ngc", [128, 8])
            cx.dma(brow[:], self.i["b_ada"][l:l + 1, :], writes=[brow.k])
            cx.dma(ngc[:], self.i["norm_g"][l].rearrange("(k p) -> p k", p=128), writes=[ngc.k],
                   allow_slow_non_contiguous=True)
            for nb in range(6):
                wbf = wbf2[nb % 2]
                self.load_w(self.i["w_ada"][l][:, nb * 512:(nb + 1) * 512], wbf, wbf.k, D, 512)
                for c in range(2):
                    b = self.bank()
                    cx.mm([(lambda e, kc=kc: e.matmul(b[0:1, :], lhsT=self.csil[:, c, kc:kc + 1], rhs=wbf[:, kc, :],
                                                      start=(kc == 0), stop=(kc == 7))) for kc in range(8)],
                          [self.csil.k, wbf.k], [b.k])
                    cx.op("dve", lambda e: e.tensor_tensor(out=modrow[0:1, c, nb * 512:(nb + 1) * 512], in0=b[0:1, :],
                                                           in1=brow[0:1, nb * 512:(nb + 1) * 512], op=ALU.add),
                          [b.k, brow.k], [modrow.k])
            b = self.bank()
            fns = []
            for c in range(2):
                for j in range(24):
                    fns.append(lambda e, c=c, j=j: e.matmul(b[:, c * 24 + j:c * 24 + j + 1],
                                                            lhsT=modrow[0:1, c, j * 128:(j + 1) * 128],
                                                            rhs=self.ones_f[0:1, 0:1], start=True, stop=True))
            cx.mm(fns, [modrow.k, self.ones_f.k], [b.k])
            cx.op("dve", lambda e: e.tensor_copy(out=self.modcol[:].rearrange("p c j -> p (c j)"), in_=b[:, 0:48]),
                  [b.k], [self.modcol.k])
            for c in range(2):
                cx.op("dve", lambda e: e.scalar_tensor_tensor(out=self.modcol[:, c, 8:16], in0=self.modcol[:, c, 8:16],
                                                              scalar=1.0, in1=ngc[:], op0=ALU.add, op1=ALU.mult),
                      [self.modcol.k, ngc.k], [self.modcol.k])
            for c in range(2):
                for hf in range(2):
                    b = self.bank()
                    cx.mm([lambda e: e.matmul(b[:, :], lhsT=self.ones_f[0:1, :],
                                              rhs=modrow[0:1, c, 2048 + hf * 512:2048 + (hf + 1) * 512],
                                              start=True, stop=True)], [modrow.k, self.ones_f.k], [b.k])
                    self.copy("act", self.gate_bc[:, c, hf * 512:(hf + 1) * 512], b[:, :], [b.k], [self.gate_bc.k])
            if "modcol" in self.o and l == self.dbg.get("layer", 0):
                cx.dma(self.o["modcol"], self.modcol[:].rearrange("p c j -> p (c j)"), reads=[self.modcol.k])
                cx.dma(self.o["gate_bc"], self.gate_bc[:].rearrange("p c j -> p (c j)"), reads=[self.gate_bc.k])
            self.end_scope(wbf2 + [modrow, brow, ngc])
            self.scope = old

    def end_scope(self, bufs):
        toks = []
        for b in bufs:
            toks += b.ka if isinstance(b, QBuf) else [b.k]
        for en in ("pe", "act", "dve", "pool", "sp"):
            self.cx.wait_all(en, toks)

    def group(self, l, t0, T, cond, kind):
        cx = self.cx
        self.hT = self.sb("hT", [128, 8, T], BF16)
        nseq, S = (self.NP, self.SP) if kind == "p" else (1, self.SS)
        self.phase1(l, t0, T, cond)
        dbg_here = (l == self.dbg.get("layer", 0) and kind == self.dbg.get("kind", "p"))
        if "hT" in self.o and dbg_here:
            tmp = self.sb("dbg_hT", [128, 8, T])
            self.copy("dve", tmp[:], self.hT[:], [self.hT.k], [tmp.k])
            cx.dma(self.o["hT"], tmp[:], reads=[tmp.k])
            self.end_scope([tmp])
        if "oT" in self.i:
            tmp = self.sb("dbg_oT", [128, 12, T])
            tmpb = self.sb("dbg_oTb", [128, 12, T], BF16)
            cx.dma(tmp[:], self.i["oT"][:, :, t0:t0 + T], writes=[tmp.k])
            self.copy("dve", tmpb[:], tmp[:], [tmp.k], [tmpb.k])
            for j in range(12):
                self.store_oT(j, t0, 0, T, tmpb[:, j, :], [tmpb.k])
            self.end_scope([tmp, tmpb])
        else:
            stages = self.dbg.get("stages", ("conv", "attn", "gdn"))
            if "gdn" in stages:
                self.phase4(l, t0, T, cond, kind, nseq, S)
            if "attn" in stages:
                self.phase2(l, t0, T, cond, kind, nseq, S)
            if "conv" in stages:
                self.phase3(l, t0, T, cond, nseq, S)
            missing = [j for br, nm in enumerate(("attn", "conv", "gdn")) if nm not in stages for j in range(br * 4, br * 4 + 4)]
            if missing:
                zt = self.sb("dbg_z", [128, 512], BF16)
                cx.op("pool", lambda e: e.memset(zt[:], 0.0), (), [zt.k])
                for j in missing:
                    for a in range(0, T, 512):
                        self.store_oT(j, t0, a, min(512, T - a), zt[:, 0:min(512, T - a)], [zt.k])
                self.end_scope([zt])
        if "oT_out" in self.o and dbg_here:
            tmpb = self.sb("dbg_oTb2", [128, 12, T], BF16)
            tmp = self.sb("dbg_oT2", [128, 12, T])
            for j in range(12):
                cx.dma(tmpb[:, j, :], self.oTs[j, :, t0:t0 + T], reads=self.oT_toks(j, t0, T), writes=[tmpb.k])
            self.copy("dve", tmp[:], tmpb[:], [tmpb.k], [tmp.k])
            cx.dma(self.o["oT_out"], tmp[:], reads=[tmp.k])
            self.end_scope([tmp, tmpb])
        self.phase5(l, t0, T, cond)
        self.end_scope([self.hT])

    def store_oT(self, j, t0, a, n, src, reads):
        key = (j, (t0 + a) // 512)
        k = self.k_oTs.setdefault(key, Tok("oTs%d_%d" % key))
        self.cx.dma(self.oTs[j, :, t0 + a:t0 + a + n], src, reads=reads, writes=[k], q=self.store_q)

    def oT_toks(self, j, r0, n):
        return [self.k_oTs[(j, b)] for b in range(r0 // 512, (r0 + n - 1) // 512 + 1) if (j, b) in self.k_oTs]

    def x_src(self, l, t0, tile):
        r0 = t0 + tile * 128
        if l == 0:
            if r0 < self.TP:
                return self.i["xp"][r0:r0 + 128, :], None
            return self.i["xs"][r0 - self.TP:r0 - self.TP + 128, :], None
        return self.xres[r0:r0 + 128, :], self.k_xres[r0]

    def phase1(self, l, t0, T, cond):
        cx = self.cx
        with ExitStack() as sc:
            old = self.scope
            self.scope = sc
            xin = [self.sb("p1_x%d" % b, [128, D]) for b in range(4)]
            junk = self.sb("p1_junk", [128, D], BF16)
            xn = [self.sb("p1_xn%d" % b, [128, D], BF16) for b in range(2)]
            ssq = [self.sb("p1_ssq%d" % b, [128, 4]) for b in range(2)]
            lnv = [self.sb("p1_lnv%d" % b, [128, 4]) for b in range(2)]
            rsd = [self.sb("p1_rsd%d" % b, [128, 4]) for b in range(2)]
            ntile = T // 128
            nblk = (ntile + 3) // 4
            for blk in range(nblk):
                tiles = list(range(blk * 4, min(ntile, blk * 4 + 4)))
                nt = len(tiles)
                banks = [self.bank() for _ in range(4)]
                sq_, ln_, rs_ = ssq[blk % 2], lnv[blk % 2], rsd[blk % 2]
                cx.op("pool", lambda e: e.memset(sq_[:], 0.0), (), [sq_.k])
                for ti, tile in enumerate(tiles):
                    xb = xin[ti]
                    src, srck = self.x_src(l, t0, tile)
                    cx.dma(xb[:], src, reads=[srck] if srck else [], writes=[xb.k])
                    cx.op("act", lambda e: e.activation(out=junk[:], in_=xb[:], func=AF.Square, accum_out=sq_[:, ti:ti + 1]),
                          [xb.k], [junk.k, sq_.k])
                cx.op("act", lambda e: e.activation(out=ln_[:, 0:nt], in_=sq_[:, 0:nt], func=AF.Ln, scale=1.0 / D, bias=EPS), [sq_.k], [ln_.k])
                cx.op("act", lambda e: e.activation(out=rs_[:, 0:nt], in_=ln_[:, 0:nt], func=AF.Exp, scale=-0.5), [ln_.k], [rs_.k])
                for ti, tile in enumerate(tiles):
                    xb = xin[ti]
                    xnb = xn[tile % 2]
                    cx.op("dve", lambda e: e.tensor_scalar(out=xnb[:], in0=xb[:], scalar1=rs_[:, ti:ti + 1], scalar2=None,
                                                           op0=ALU.mult), [xb.k, rs_.k], [xnb.k])
                    for fc in range(8):
                        b = banks[fc // 2]
                        pv = b.t[:].bitcast(BF16)
                        off = ((fc % 2) * 4 + ti) * 128
                        cx.mm([lambda e: e.transpose(out=pv[:, off:off + 128], in_=xnb[:, fc * 128:(fc + 1) * 128],
                                                     identity=self.ident[:])],
                              [xnb.k, self.ident.k], [b.k])
                n = len(tiles) * 128
                for fc in range(8):
                    b = banks[fc // 2]
                    pv = b.t[:].bitcast(BF16)
                    off = (fc % 2) * 512
                    dst = self.hT[:, fc, blk * 512:blk * 512 + n]
                    if fc % 2 == 0:
                        cx.op("act", lambda e: e.activation(out=dst, in_=pv[:, off:off + n], func=AF.Identity,
                                                            scale=self.modcol[:, cond, 8 + fc:9 + fc],
                                                            bias=self.modcol[:, cond, fc:fc + 1]),
                              [b.k, self.modcol.k], [self.hT.k])
                    else:
                        cx.op("dve", lambda e: e.tensor_scalar(out=dst, in0=pv[:, off:off + n],
                                                               scalar1=self.modcol[:, cond, 8 + fc:9 + fc],
                                                               scalar2=self.modcol[:, cond, fc:fc + 1],
                                                               op0=ALU.mult, op1=ALU.add),
                              [b.k, self.modcol.k], [self.hT.k])
            self.end_scope(xin + xn + [junk] + ssq + lnv + rsd)
            self.scope = old

    def phase2(self, l, t0, T, cond, kind, nseq, S):
        cx = self.cx
        P = 0 if kind == "p" else self.PAST
        rope = (kind == "s")
        Sk = P + S
        nkt = Sk // 128
        sc96 = 96.0 ** -0.5
        self.set_pools({"bo": [0, 1], "sc": [2, 3, 4], "m": [5, 6, 7]})
        with ExitStack() as sc:
            old = self.scope
            self.scope = sc
            wuq = self.sb("p2_wuq", [128, 3, 768], BF16)
            wukv = self.sb("p2_wukv", [128, 2, 1024], BF16)
            wga = self.sb("p2_wga", [128, 8, 512], BF16)
            qg = self.sb("p2_qg", [128, 3])
            gkv = self.sb("p2_gkv", [128, 256])
            cqnT = self.sb("p2_cqnT", [128, 3, T], BF16)
            ckvnT = self.sb("p2_ckvnT", [128, 2, nseq, Sk], BF16)
            KTr = self.sb("p2_KTr", [33, nseq, Sk], BF16)
            Vaug = self.sb("p2_V", [128, nkt, 8, 65], BF16)
            KTf = self.sb("p2_KTf", [128, 8, Sk], BF16)
            onesm = self.sb("p2_onesm", [128, 1], BF16)
            if rope:
                wuqP = self.sb("p2_wuqP", [128, 3, 768], BF16)
                cosT = self.sb("p2_cos", [32, S])
                sinT = self.sb("p2_sin", [32, S])
                rt = [self.sb("p2_rt%d" % b, [32, 512]) for b in range(2)]
            sA = ExitStack()
            sA.__enter__()
            self.scope = sA
            wA = self.sb("p2_wA", [128, 8, 672], BF16)
            ones_b = self.sb("p2_ones", [128, 1], BF16)
            st = self.sb("p2_st", [128, 8])
            junk = self.sb("p2_junk", [128, 384], BF16)
            cqn = [self.sb("p2_cqn%d" % b, [128, 384], BF16) for b in range(2)]
            ckvf = [self.sb("p2_ckvf%d" % b, [128, 256]) for b in range(2)]
            ckvb = [self.sb("p2_ckvb%d" % b, [128, 256], BF16) for b in range(2)]
            kpef = [self.sb("p2_kpef%d" % b, [128, 32]) for b in range(2)]
            kpeb = self.sb("p2_kpeb", [128, 32], BF16)
            if rope:
                wkP = self.sb("p2_wkP", [128, 8, 32], BF16)
            cx.op("pool", lambda e: e.memset(ones_b[:], 1.0), (), [ones_b.k])
            cx.dma(qg[:], self.i["q_norm_g"][l].rearrange("(k p) -> p k", p=128), writes=[qg.k],
                   allow_slow_non_contiguous=True)
            cx.dma(gkv[:], self.i["kv_norm_g"][l].partition_broadcast(128), writes=[gkv.k])
            self.load_w(self.i["w_in"][l][:, 0:672], wA, wA.k, D, 672)
            self.load_w(self.i["w_uq"][l], wuq, wuq.k, 384, 768, row_scale=qg, extra_reads=[qg.k])
            self.load_w(self.i["w_ukv"][l], wukv, wukv.k, 256, 1024)
            self.load_w(self.i["w_in"][l][:, C_GA:C_GA + 512], wga, wga.k, D, 512)
            self.stop_at("p2_w")
            cx.op("pool", lambda e: e.memset(KTr[32:33, :, :], 1.0), (), [KTr.k])
            cx.op("pool", lambda e: e.memset(Vaug[:, :, :, 64:65], 1.0), (), [Vaug.k])
            if rope:
                cx.dma(cosT[:], self.i["k_rope"][0, 64:96, :], writes=[cosT.k])
                cx.dma(sinT[:], self.i["k_rope"][1, 64:96, :], writes=[sinT.k])
                cx.op("pool", lambda e: e.memset(wuqP[:], 0.0), (), [wuqP.k])
                wv = wuq[:].rearrange("p k (h c) -> p k h c", h=8)[:, :, :, 64:96].rearrange("p k h (i t) -> p k h i t", t=2)
                pv = wuqP[:].rearrange("p k (h c) -> p k h c", h=8)[:, :, :, 64:96].rearrange("p k h (i t) -> p k h i t", t=2)
                for kc in range(3):
                    cx.op("dve", lambda e: e.tensor_scalar(out=pv[:, kc, :, :, 0], in0=wv[:, kc, :, :, 1], scalar1=-1.0,
                                                           scalar2=None, op0=ALU.mult), [wuq.k], [wuqP.k])
                    cx.op("dve", lambda e: e.tensor_copy(out=pv[:, kc, :, :, 1], in_=wv[:, kc, :, :, 0]), [wuq.k], [wuqP.k])
                kv_ = wA[:, :, 640:672].rearrange("p k (i t) -> p k i t", t=2)
                kp_ = wkP[:].rearrange("p k (i t) -> p k i t", t=2)
                cx.op("dve", lambda e: e.tensor_scalar(out=kp_[:, :, :, 0], in0=kv_[:, :, :, 1], scalar1=-1.0, scalar2=None,
                                                       op0=ALU.mult), [wA.k], [wkP.k])
                cx.op("dve", lambda e: e.tensor_copy(out=kp_[:, :, :, 1], in_=kv_[:, :, :, 0]), [wA.k], [wkP.k])
            self.stop_at("p2_w2")
            def stageA_mm(tile):
                ts_ = slice(tile * 128, (tile + 1) * 128)
                b1_ = self.bank("sc")
                cx.mm([(lambda e, kc=kc: e.matmul(b1_[:, 0:384], lhsT=self.hT[:, kc, ts_], rhs=wA[:, kc, 0:384],
                                                  start=(kc == 0), stop=(kc == 7))) for kc in range(8)],
                      [self.hT.k, wA.k], [b1_.k])
                b2_ = self.bank("bo")
                cx.mm([(lambda e, kc=kc: e.matmul(b2_[:, 0:288], lhsT=self.hT[:, kc, ts_], rhs=wA[:, kc, 384:672],
                                                  start=(kc == 0), stop=(kc == 7))) for kc in range(8)],
                      [self.hT.k, wA.k], [b2_.k])
                return b1_, b2_
            nxt = stageA_mm(0)
            for tile in range(T // 128):
                q, pos = (tile * 128) // S, (tile * 128) % S
                ts = slice(tile * 128, (tile + 1) * 128)
                b1, b2 = nxt
                if tile + 1 < T // 128:
                    nxt = stageA_mm(tile + 1)
                cqb, cf, cb_ = cqn[tile % 2], ckvf[tile % 2], ckvb[tile % 2]
                cx.op("pool", lambda e: e.memset(st[:, 0:8], 0.0), (), [st.k])
                cx.op("act", lambda e: e.activation(out=junk[:, 0:384], in_=b1[:, 0:384], func=AF.Square,
                                                    accum_out=st[:, 0:1]), [b1.k], [junk.k, st.k])
                cx.op("act", lambda e: e.activation(out=junk[:, 0:256], in_=b2[:, 0:256], func=AF.Square,
                                                    accum_out=st[:, 4:5]), [b2.k], [junk.k, st.k])
                self.rsqrt_mean(st[:, 0:1], st[:, 2:3], 384, [st.k], [st.k], st[:, 1:2])
                self.rsqrt_mean(st[:, 4:5], st[:, 6:7], 256, [st.k], [st.k], st[:, 5:6])
                cx.op("dve", lambda e: e.tensor_scalar(out=cqb[:], in0=b1[:, 0:384], scalar1=st[:, 2:3], scalar2=None,
                                                       op0=ALU.mult), [b1.k, st.k], [cqb.k])
                cx.op("dve", lambda e: e.scalar_tensor_tensor(out=cf[:], in0=b2[:, 0:256], scalar=st[:, 6:7], in1=gkv[:],
                                                              op0=ALU.mult, op1=ALU.mult), [b2.k, st.k, gkv.k], [cf.k])
                self.copy("pool", cb_[:], cf[:], [cf.k], [cb_.k])
                if kind == "p":
                    kf = kpef[tile % 2]
                    self.copy("act", kf[:], b2[:, 256:288], [b2.k], [kf.k])
                    self.out_dma(self.o["nckv"][q, l, pos:pos + 128, :], cf[:], [cf.k])
                    self.out_dma(self.o["nkpe"][q, l, pos:pos + 128, :], kf[:], [kf.k])
                b3 = self.bank("m")
                pv3 = b3.t[:].bitcast(BF16)
                for kc in range(3):
                    cx.mm([lambda e: e.transpose(out=pv3[:, kc * 128:(kc + 1) * 128], in_=cqb[:, kc * 128:(kc + 1) * 128],
                                                 identity=self.ident[:])], [cqb.k, self.ident.k], [b3.k])
                for kc in range(2):
                    cx.mm([lambda e: e.transpose(out=pv3[:, (3 + kc) * 128:(4 + kc) * 128],
                                                 in_=cb_[:, kc * 128:(kc + 1) * 128], identity=self.ident[:])],
                          [cb_.k, self.ident.k], [b3.k])
                self.copy("act", cqnT[:, :, ts], pv3[:, 0:384].rearrange("p (k n) -> p k n", k=3), [b3.k], [cqnT.k])
                self.copy("dve", ckvnT[:, :, q, P + pos:P + pos + 128],
                          pv3[:, 384:640].rearrange("p (k n) -> p k n", k=2), [b3.k], [ckvnT.k])
            self.stop_at("p2_A")
            for ct in range(P // 128):
                cf, cb_ = ckvf[ct % 2], ckvb[ct % 2]
                kf = kpef[ct % 2]
                cx.dma(cf[:], self.i["cckv"][l, ct * 128:(ct + 1) * 128, :], writes=[cf.k])
                cx.dma(kf[:], self.i["ckpe"][l, ct * 128:(ct + 1) * 128, :], writes=[kf.k])
                self.copy("pool", cb_[:], cf[:], [cf.k], [cb_.k])
                self.copy("pool", kpeb[:], kf[:], [kf.k], [kpeb.k])
                b3 = self.bank("m")
                pv3 = b3.t[:].bitcast(BF16)
                for kc in range(2):
                    cx.mm([lambda e: e.transpose(out=pv3[:, kc * 128:(kc + 1) * 128], in_=cb_[:, kc * 128:(kc + 1) * 128],
                                                 identity=self.ident[:])], [cb_.k, self.ident.k], [b3.k])
                cx.mm([lambda e: e.transpose(out=pv3[0:32, 256:384], in_=kpeb[:, 0:32], identity=self.ident[:])],
                      [kpeb.k, self.ident.k], [b3.k])
                self.copy("dve", ckvnT[:, :, 0, ct * 128:(ct + 1) * 128],
                          pv3[:, 0:256].rearrange("p (k n) -> p k n", k=2), [b3.k], [ckvnT.k])
                self.copy("act", KTr[0:32, 0, ct * 128:(ct + 1) * 128], pv3[0:32, 256:384], [b3.k], [KTr.k])
            self.stop_at("p2_ctx")
            for blk in range((T + 511) // 512):
                n = min(512, T - blk * 512)
                ts = slice(blk * 512, blk * 512 + n)
                ba = self.bank("m")
                cx.mm([(lambda e, kc=kc: e.matmul(ba[0:32, 0:n], lhsT=wA[:, kc, 640:672], rhs=self.hT[:, kc, ts],
                                                  start=(kc == 0), stop=(kc == 7))) for kc in range(8)],
                      [wA.k, self.hT.k], [ba.k])
                dst, k = self.seq_view(KTr[0:32], blk * 512, n, S, P)
                if not rope:
                    src = ba[0:32, 0:n]
                    if k is not None:
                        src = src.rearrange("p (k s) -> p k s", k=k)
                    self.copy("act", dst, src, [ba.k], [KTr.k])
                else:
                    bb = self.bank("m")
                    cx.mm([(lambda e, kc=kc: e.matmul(bb[0:32, 0:n], lhsT=wkP[:, kc, :], rhs=self.hT[:, kc, ts],
                                                      start=(kc == 0), stop=(kc == 7))) for kc in range(8)],
                          [wkP.k, self.hT.k], [bb.k])
                    cx.op("dve", lambda e: e.tensor_tensor(out=rt[0][:, 0:n], in0=ba[0:32, 0:n], in1=cosT[:, ts], op=ALU.mult),
                          [ba.k, cosT.k], [rt[0].k])
                    cx.op("dve", lambda e: e.tensor_tensor(out=rt[1][:, 0:n], in0=bb[0:32, 0:n], in1=sinT[:, ts], op=ALU.mult),
                          [bb.k, sinT.k], [rt[1].k])
                    cx.op("pool", lambda e: e.tensor_tensor(out=dst, in0=rt[0][:, 0:n], in1=rt[1][:, 0:n], op=ALU.add),
                          [rt[0].k, rt[1].k], [KTr.k])
            self.stop_at("p2_kr")
            self.end_scope([wA, ones_b, st, junk, kpeb] + cqn + ckvf + ckvb + kpef + ([wkP] if rope else []))
            sA.__exit__(None, None, None)
            self.scope = sc
            ksq = self.sb("p2_ksq", [128, Sk], BF16)
            kmx = self.sb("p2_kmx", [1, 16])
            QTf = [self.sb("p2_QTf%d" % b, [128, 512], BF16) for b in range(2)]
            qsq = [self.sb("p2_qsq%d" % b, [128, 512], BF16) for b in range(1)]
            bnd = self.sb("p2_bnd", [1, 512])
            PT = [self.sb("p2_PT%d" % b, [128, 512], BF16) for b in range(2)]
            otm = self.sb("p2_otm", [128, 4, 512])
            rec = self.sb("p2_rec", [128, 4, 1])
            sgt = self.sb("p2_sg", [128, 512])
            ogb = self.sb("p2_og", [128, 512], BF16)
            oab = [self.sb("p2_oa0", [128, 4, 512], BF16)]
            cx.op("pool", lambda e: e.memset(onesm[:], 1.0), (), [onesm.k])
            cx.op("pool", lambda e: e.memset(onesm[32:33, :], 0.0), (), [onesm.k])
            cx.op("pool", lambda e: e.memset(KTf[32:64, :, :], 0.0), (), [KTf.k])
            for b_ in range(2):
                cx.op("pool", lambda e: e.memset(QTf[b_][32:64, :], 0.0), (), [QTf[b_].k])
            wv_v = [wukv[:, kc, :].rearrange("p (h t d) -> p h t d", h=8, t=2)[:, :, 1, :] for kc in range(2)]
            pt_i = 0
            qi = 0
            for q in range(nseq):
                for h in range(8):
                    self.copy(("act", "dve", "pool")[h % 3], KTf[0:33, h, :], KTr[0:33, q, :], [KTr.k], [KTf.k])
                    for kb in range((Sk + 511) // 512):
                        n = min(512, Sk - kb * 512)
                        ks = slice(kb * 512, kb * 512 + n)
                        b = self.bank("m")
                        cx.mm([(lambda e, kc=kc: e.matmul(b[64:128, 0:n], lhsT=wukv[:, kc, h * 128:h * 128 + 64],
                                                          rhs=ckvnT[:, kc, q, ks], start=(kc == 0), stop=(kc == 1)))
                               for kc in range(2)], [wukv.k, ckvnT.k], [b.k])
                        self.copy(("act", "dve")[kb % 2], KTf[64:128, h, ks], b[64:128, 0:n], [b.k], [KTf.k])
                for kt in range(nkt):
                    b = self.bank("m")
                    cx.mm([(lambda e, kc=kc: e.matmul(b[:, :].rearrange("p (h d) -> p h d", h=8),
                                                      lhsT=ckvnT[:, kc, q, kt * 128:(kt + 1) * 128], rhs=wv_v[kc],
                                                      start=(kc == 0), stop=(kc == 1))) for kc in range(2)],
                          [wukv.k, ckvnT.k], [b.k])
                    self.copy(("act", "dve")[kt % 2], Vaug[:, kt, :, 0:64], b[:, :].rearrange("p (h d) -> p h d", h=8),
                              [b.k], [Vaug.k])
                self.stop_at("p2_kv")
                for h in range(8):
                    cx.op("act", lambda e: e.activation(out=ksq[:, :], in_=KTf[:, h, :], func=AF.Square), [KTf.k], [ksq.k])
                    for kb in range((Sk + 511) // 512):
                        n = min(512, Sk - kb * 512)
                        b = self.bank("m")
                        cx.mm([lambda e: e.matmul(b[0:1, 0:n], lhsT=onesm[:, 0:1], rhs=ksq[:, kb * 512:kb * 512 + n],
                                                  start=True, stop=True)], [onesm.k, ksq.k], [b.k])
                        if kb == 0:
                            cx.op("dve", lambda e: e.reduce_max(out=kmx[0:1, h:h + 1], in_=b[0:1, 0:n], axis=AX.X), [b.k], [kmx.k])
                        else:
                            cx.op("dve", lambda e: e.reduce_max(out=kmx[0:1, 15:16], in_=b[0:1, 0:n], axis=AX.X), [b.k], [kmx.k])
                            cx.op("dve", lambda e: e.tensor_tensor(out=kmx[0:1, h:h + 1], in0=kmx[0:1, h:h + 1],
                                                                   in1=kmx[0:1, 15:16], op=ALU.max), [kmx.k], [kmx.k])
                self.stop_at("p2_km")
                items = [(a, h) for a in range(0, S, 512) for h in range(8)]

                def build_a(idx):
                    a, h = items[idx]
                    n = min(512, S - a)
                    ts = slice(q * S + a, q * S + a + n)
                    qf, sqf = QTf[idx % 2], qsq[0]
                    b = self.bank("m")
                    cx.mm([(lambda e, kc=kc: e.matmul(b[64:128, 0:n], lhsT=wuq[:, kc, h * 96:h * 96 + 64], rhs=cqnT[:, kc, ts],
                                                      start=(kc == 0), stop=(kc == 2))) for kc in range(3)],
                          [wuq.k, cqnT.k], [b.k])
                    self.copy("dve", qf[64:128, 0:n], b[64:128, 0:n], [b.k], [qf.k])
                    ba = self.bank("m")
                    cx.mm([(lambda e, kc=kc: e.matmul(ba[0:32, 0:n], lhsT=wuq[:, kc, h * 96 + 64:h * 96 + 96],
                                                      rhs=cqnT[:, kc, ts], start=(kc == 0), stop=(kc == 2)))
                           for kc in range(3)], [wuq.k, cqnT.k], [ba.k])
                    if not rope:
                        self.copy("act", qf[0:32, 0:n], ba[0:32, 0:n], [ba.k], [qf.k])
                    else:
                        bb = self.bank("m")
                        cx.mm([(lambda e, kc=kc: e.matmul(bb[0:32, 0:n], lhsT=wuqP[:, kc, h * 96 + 64:h * 96 + 96],
                                                          rhs=cqnT[:, kc, ts], start=(kc == 0), stop=(kc == 2)))
                               for kc in range(3)], [wuqP.k, cqnT.k], [bb.k])
                        ps_ = slice(a, a + n)
                        cx.op("dve", lambda e: e.tensor_tensor(out=rt[0][:, 0:n], in0=ba[0:32, 0:n], in1=cosT[:, ps_],
                                                               op=ALU.mult), [ba.k, cosT.k], [rt[0].k])
                        cx.op("dve", lambda e: e.tensor_tensor(out=rt[1][:, 0:n], in0=bb[0:32, 0:n], in1=sinT[:, ps_],
                                                               op=ALU.mult), [bb.k, sinT.k], [rt[1].k])
                        cx.op("pool", lambda e: e.tensor_tensor(out=qf[0:32, 0:n], in0=rt[0][:, 0:n], in1=rt[1][:, 0:n],
                                                                op=ALU.add), [rt[0].k, rt[1].k], [qf.k])
                    cx.op("dve", lambda e: e.tensor_tensor(out=sqf[:, 0:n], in0=qf[:, 0:n], in1=qf[:, 0:n], op=ALU.mult), [qf.k], [sqf.k])

                def build_b(idx):
                    a, h = items[idx]
                    n = min(512, S - a)
                    qf, sqf = QTf[idx % 2], qsq[0]
                    bq = self.bank("m")
                    cx.mm([lambda e: e.matmul(bq[0:1, 0:n], lhsT=onesm[:, 0:1], rhs=sqf[:, 0:n], start=True, stop=True)],
                          [onesm.k, sqf.k], [bq.k])
                    cx.op("act", lambda e: e.activation(out=bnd[0:1, 0:n], in_=bq[0:1, 0:n], func=AF.Ln,
                                                        scale=kmx[0:1, h:h + 1], bias=1e-30), [bq.k, kmx.k], [bnd.k])
                    cx.op("act", lambda e: e.activation(out=bnd[0:1, 0:n], in_=bnd[0:1, 0:n], func=AF.Exp, scale=0.5), [bnd.k], [bnd.k])
                    cx.op("dve", lambda e: e.tensor_scalar(out=qf[32:33, 0:n], in0=bnd[0:1, 0:n], scalar1=-1.0, scalar2=None,
                                                           op0=ALU.mult), [bnd.k], [qf.k])

                build_a(0)
                build_b(0)
                for idx, (a, h) in enumerate(items):
                    n = min(512, S - a)
                    nq = n // 128
                    g0 = q * S + a
                    qf = QTf[idx % 2]
                    have_next = idx + 1 < len(items)
                    if have_next:
                        build_a(idx + 1)
                    bo = self.bank("bo")
                    banks = {}

                    def score(kt):
                        bs = self.bank("sc")
                        banks[kt] = bs
                        cx.mm([lambda e: e.matmul(bs[:, 0:n], lhsT=KTf[:, h, kt * 128:(kt + 1) * 128], rhs=qf[:, 0:n], start=True, stop=True)],
                              [KTf.k, qf.k], [bs.k])
                    score(0)
                    if nkt > 1:
                        score(1)
                    for kt in range(nkt):
                        if kt + 2 < nkt:
                            score(kt + 2)
                        if have_next and kt == min(nkt - 1, max(0, nkt // 2 - 1)):
                            build_b(idx + 1)
                        bs = banks.pop(kt)
                        pt = PT[pt_i % 2]
                        pt_i += 1
                        cx.op("act", lambda e: e.activation(out=pt[:, 0:n], in_=bs[:, 0:n], func=AF.Exp, scale=sc96),
                              [bs.k], [pt.k])
                        cx.mm([(lambda e, qs=qs: e.matmul(bo[:, qs * 65:(qs + 1) * 65], lhsT=pt[:, qs * 128:(qs + 1) * 128],
                                                          rhs=Vaug[:, kt, h, :], start=(kt == 0 and qs == 0), stop=(kt == nkt - 1),
                                                          skip_group_check=True))
                               for qs in range(nq)], [pt.k, Vaug.k], [bo.k])
                    bov = bo[:, 0:nq * 65].rearrange("p (s c) -> p s c", c=65)
                    cx.op("dve", lambda e: e.reciprocal(out=rec[:, 0:nq, :], in_=bov[:, :, 64:65]), [bo.k], [rec.k])
                    cx.op("dve", lambda e: e.tensor_tensor(out=otm[:, 0:nq, h * 64:(h + 1) * 64], in0=bov[:, :, 0:64],
                                                           in1=rec[:, 0:nq, :].to_broadcast([128, nq, 64]), op=ALU.mult),
                          [bo.k, rec.k], [otm.k])
                    if h != 7:
                        continue
                    oa = oab[0]
                    for qs in range(nq):
                        tsl = slice(g0 + qs * 128, g0 + (qs + 1) * 128)
                        bg_ = self.bank("m")
                        cx.mm([(lambda e, kc=kc: e.matmul(bg_[:, :], lhsT=self.hT[:, kc, tsl], rhs=wga[:, kc, :],
                                                          start=(kc == 0), stop=(kc == 7))) for kc in range(8)],
                              [self.hT.k, wga.k], [bg_.k])
                        cx.op("act", lambda e: e.activation(out=sgt[:], in_=bg_[:, :], func=AF.Silu), [bg_.k], [sgt.k])
                        cx.op("dve", lambda e: e.tensor_tensor(out=ogb[:], in0=otm[:, qs, :], in1=sgt[:], op=ALU.mult),
                              [otm.k, sgt.k], [ogb.k])
                        bt = self.bank("m")
                        ptv = bt.t[:].bitcast(BF16)
                        for j in range(4):
                            cx.mm([lambda e: e.transpose(out=ptv[:, j * 128:(j + 1) * 128], in_=ogb[:, j * 128:(j + 1) * 128],
                                                         identity=self.ident[:])], [ogb.k, self.ident.k], [bt.k])
                        self.copy("act", oa[:, :, qs * 128:(qs + 1) * 128],
                                  ptv[:, 0:512].rearrange("p (j n) -> p j n", j=4), [bt.k], [oa.k])
                    for j in range(4):
                        self.store_oT(j, t0, g0, n, oa[:, j, 0:n], [oa.k])
            bufs = [wuq, wukv, wga, qg, gkv, cqnT, ckvnT, KTr, KTf, onesm, Vaug, ksq, kmx, bnd, otm,
                    rec, sgt, ogb] + QTf + qsq + PT + oab
            if rope:
                bufs += [wuqP, cosT, sinT] + rt
            self.end_scope(bufs)
            self.scope = old

    def seq_view(self, buf3, a, n, S, off):
        if n <= S:
            q, pos = a // S, a % S
            return buf3[:, q, off + pos:off + pos + n], None
        k = n // S
        return buf3[:, a // S:a // S + k, off:off + S], k

    def phase3(self, l, t0, T, cond, nseq, S):
        cx = self.cx
        with ExitStack() as sc:
            old = self.scope
            self.scope = sc
            wc2 = [self.sb("p3_w%d" % b, [128, 8, 4, 128], BF16) for b in range(2)]
            cw = self.sb("p3_cw", [128, 4, 3])
            cxp = self.sb("p3_cx", [128, nseq, S + 2])
            bg = self.sb("p3_bg", [128, T])
            y = self.sb("p3_y", [128, T])
            yb = self.sb("p3_yb", [128, T], BF16)
            t1 = [self.sb("p3_t%d" % b, [128, 512]) for b in range(2)]
            for tap in range(3):
                cx.dma(cw[:, :, tap], self.i["conv_b_w"][l][tap].rearrange("(c p) -> p c", p=128), writes=[cw.k],
                       allow_slow_non_contiguous=True)
            cx.op("pool", lambda e: e.memset(cxp[:], 0.0), (), [cxp.k])
            nblk = (T + 511) // 512
            cols = [C_B, C_C, C_X, C_GB]
            y3 = y[:].rearrange("p (q s) -> p q s", q=nseq)
            for cb in range(4):
                wc = wc2[cb % 2]
                for wi in range(4):
                    self.load_w(self.i["w_in"][l][:, cols[wi] + cb * 128:cols[wi] + (cb + 1) * 128],
                                wc[:, :, wi, :], wc.k, D, 128)
                for blk in range(nblk):
                    n = min(512, T - blk * 512)
                    ts = slice(blk * 512, blk * 512 + n)
                    bk = []
                    for wi in range(4):
                        b = self.bank()
                        cx.mm([(lambda e, kc=kc: e.matmul(b[:, 0:n], lhsT=wc[:, kc, wi, :], rhs=self.hT[:, kc, ts],
                                                          start=(kc == 0), stop=(kc == 7))) for kc in range(8)],
                              [wc.k, self.hT.k], [b.k])
                        bk.append(b)
                    pb, pc, px, pg = bk
                    self.copy("act", t1[0][:, 0:n], pc[:, 0:n], [pc.k], [t1[0].k])
                    dst, k = self.seq_view(cxp, blk * 512, n, S, 1)
                    src0, src1 = px[:, 0:n], t1[0][:, 0:n]
                    if k is not None:
                        src0 = src0.rearrange("p (k s) -> p k s", k=k)
                        src1 = src1.rearrange("p (k s) -> p k s", k=k)
                    cx.op("dve", lambda e: e.tensor_tensor(out=dst, in0=src0, in1=src1, op=ALU.mult),
                          [px.k, t1[0].k], [cxp.k])
                    cx.op("act", lambda e: e.activation(out=t1[1][:, 0:n], in_=pg[:, 0:n], func=AF.Silu),
                          [pg.k], [t1[1].k])
                    cx.op("dve", lambda e: e.tensor_tensor(out=bg[:, ts], in0=pb[:, 0:n], in1=t1[1][:, 0:n], op=ALU.mult),
                          [pb.k, t1[1].k], [bg.k])
                cx.op("act", lambda e: e.activation(out=y3, in_=cxp[:, :, 0:S], func=AF.Copy, scale=cw[:, cb, 0:1]),
                      [cxp.k, cw.k], [y.k])
                cx.op("dve", lambda e: e.scalar_tensor_tensor(out=y3, in0=cxp[:, :, 1:S + 1], scalar=cw[:, cb, 1:2], in1=y3,
                                                               op0=ALU.mult, op1=ALU.add), [cxp.k, cw.k, y.k], [y.k])
                cx.op("dve", lambda e: e.scalar_tensor_tensor(out=y3, in0=cxp[:, :, 2:S + 2], scalar=cw[:, cb, 2:3], in1=y3,
                                                               op0=ALU.mult, op1=ALU.add), [cxp.k, cw.k, y.k], [y.k])
                cx.op("dve", lambda e: e.tensor_tensor(out=yb[:], in0=y[:], in1=bg[:], op=ALU.mult), [y.k, bg.k], [yb.k])
                for blk in range(nblk):
                    n = min(512, T - blk * 512)
                    self.store_oT(4 + cb, t0, blk * 512, n, yb[:, blk * 512:blk * 512 + n], [yb.k])
            self.end_scope([cw, cxp, bg, y, yb] + wc2 + t1)
            self.scope = old

    def phase4(self, l, t0, T, cond, kind, nseq, S):
        cx = self.cx
        ntile = T // 128
        tps = S // 128
        self.set_pools({"a": [0, 1, 2, 3], "b": [4, 5], "c": [6, 7]})
        with ExitStack() as sc:
            old = self.scope
            self.scope = sc
            qT = self.sb("p4_qT", [128, 4, T], BF16)
            kT = self.sb("p4_kT", [128, 4, T], BF16)
            vT = self.sb("p4_vT", [128, 4, T], BF16)
            gall = self.sb("p4_g", [128, ntile, 16])
            ball = self.sb("p4_b", [128, ntile, 16])
            wab = self.sb("p4_wab", [128, 8, 32], BF16)
            mk = self.sb("p4_mk", [128, 4, 128])
            bd1 = self.sb("p4_bd1", [128, 128], BF16)
            nA = self.sb("p4_nA", [128, 16])
            dtb = self.sb("p4_dtb", [128, 16])
            gn = self.sb("p4_gn", [128, 64])
            for m_ in range(4):
                cx.dma(mk[:, m_, :], self.i["k_mask"][m_], writes=[mk.k])
            cx.op("pool", lambda e: e.memset(bd1[:], 0.0), (), [bd1.k])
            cx.op("pool", lambda e: e.memset(bd1[0:64, 0:64], 1.0), (), [bd1.k])
            cx.op("pool", lambda e: e.memset(bd1[64:128, 64:128], 1.0), (), [bd1.k])
            cx.dma(nA[:], self.i["a_log"][l].partition_broadcast(128), writes=[nA.k])
            cx.dma(dtb[:], self.i["dt_bias"][l].partition_broadcast(128), writes=[dtb.k])
            cx.dma(gn[:], self.i["gdn_norm_g"][l].partition_broadcast(128), writes=[gn.k])
            cx.op("act", lambda e: e.activation(out=nA[:], in_=nA[:], func=AF.Exp), [nA.k], [nA.k])
            cx.op("dve", lambda e: e.tensor_scalar(out=nA[:], in0=nA[:], scalar1=-1.0, scalar2=None, op0=ALU.mult), [nA.k], [nA.k])
            self.load_w(self.i["w_in"][l][:, C_AB:C_AB + 32], wab, wab.k, D, 32)
            with ExitStack() as sa:
                self.scope = sa
                wqkv2 = [self.sb("p4_wqkv%d" % b, [128, 8, 512], BF16) for b in range(2)]
                cqw = self.sb("p4_cqw", [128, 12, 3])
                rawp2 = [self.sb("p4_raw%d" % b, [128, nseq, S + 2]) for b in range(2)]
                y2 = [self.sb("p4_y%d" % b, [128, T]) for b in range(2)]
                sl2 = [self.sb("p4_sl%d" % b, [128, T]) for b in range(2)]
                sq = self.sb("p4_sq", [128, T], BF16)
                tmpn = [self.sb("p4_tn%d" % b, [128, 512]) for b in range(2)]
                for tap in range(3):
                    cx.dma(cqw[:, :, tap], self.i["conv_qkv_w"][l][tap].rearrange("(c p) -> p c", p=128), writes=[cqw.k],
                           allow_slow_non_contiguous=True)
                for rp_ in rawp2:
                    cx.op("pool", lambda e: e.memset(rp_[:], 0.0), (), [rp_.k])
                nblk = (T + 511) // 512
                for X, (c0, dstT) in enumerate(((C_Q, qT), (C_K, kT), (C_V, vT))):
                    wqkv = wqkv2[X % 2]
                    self.load_w(self.i["w_in"][l][:, c0:c0 + 512], wqkv, wqkv.k, D, 512)
                    for hp in range(4):
                        cc = X * 4 + hp
                        rawp, y, sl = rawp2[cc % 2], y2[cc % 2], sl2[cc % 2]
                        y3 = y[:].rearrange("p (q s) -> p q s", q=nseq)
                        for blk in range(nblk):
                            n = min(512, T - blk * 512)
                            ts = slice(blk * 512, blk * 512 + n)
                            b = self.bank()
                            cx.mm([(lambda e, kc=kc: e.matmul(b[:, 0:n], lhsT=wqkv[:, kc, hp * 128:(hp + 1) * 128],
                                                              rhs=self.hT[:, kc, ts], start=(kc == 0), stop=(kc == 7)))
                                   for kc in range(8)], [wqkv.k, self.hT.k], [b.k])
                            dst, k = self.seq_view(rawp, blk * 512, n, S, 1)
                            src = b[:, 0:n]
                            if k is not None:
                                src = src.rearrange("p (k s) -> p k s", k=k)
                            self.copy(("act", "dve")[blk % 2], dst, src, [b.k], [rawp.k])
                        cx.op("act", lambda e: e.activation(out=y3, in_=rawp[:, :, 0:S], func=AF.Copy, scale=cqw[:, cc, 0:1]),
                              [rawp.k, cqw.k], [y.k])
                        cx.op("dve", lambda e: e.scalar_tensor_tensor(out=y3, in0=rawp[:, :, 1:S + 1], scalar=cqw[:, cc, 1:2],
                                                                       in1=y3, op0=ALU.mult, op1=ALU.add), [rawp.k, cqw.k, y.k], [y.k])
                        cx.op("dve", lambda e: e.scalar_tensor_tensor(out=y3, in0=rawp[:, :, 2:S + 2], scalar=cqw[:, cc, 2:3],
                                                                       in1=y3, op0=ALU.mult, op1=ALU.add), [rawp.k, cqw.k, y.k], [y.k])
                        cx.op("act", lambda e: e.activation(out=sl[:], in_=y[:], func=AF.Silu), [y.k], [sl.k])
                        if X == 2:
                            self.copy("dve", dstT[:, hp, :], sl[:], [sl.k], [dstT.k])
                            continue
                        cx.op("act", lambda e: e.activation(out=sq[:], in_=sl[:], func=AF.Square), [sl.k], [sq.k])
                        for blk in range(nblk):
                            n = min(512, T - blk * 512)
                            ts = slice(blk * 512, blk * 512 + n)
                            b = self.bank()
                            tn = tmpn[blk % 2]
                            cx.mm([lambda e: e.matmul(b[:, 0:n], lhsT=bd1[:], rhs=sq[:, ts], start=True, stop=True)],
                                  [bd1.k, sq.k], [b.k])
                            cx.op("act", lambda e: e.activation(out=tn[:, 0:n], in_=b[:, 0:n], func=AF.Ln, bias=EPS), [b.k], [tn.k])
                            cx.op("act", lambda e: e.activation(out=tn[:, 0:n], in_=tn[:, 0:n], func=AF.Exp, scale=-0.5), [tn.k], [tn.k])
                            cx.op("dve", lambda e: e.scalar_tensor_tensor(out=dstT[:, hp, ts], in0=sl[:, ts],
                                                                          scalar=(0.125 if X == 0 else 1.0), in1=tn[:, 0:n],
                                                                          op0=ALU.mult, op1=ALU.mult), [sl.k, tn.k], [dstT.k])
                self.end_scope([cqw, sq] + rawp2 + y2 + sl2 + wqkv2 + tmpn)
            self.scope = sc
            t16 = [self.sb("p4_t16%d" % b, [128, 16]) for b in range(4)]
            for tile in range(ntile):
                ts = slice(tile * 128, (tile + 1) * 128)
                b = self.bank()
                cx.mm([(lambda e, kc=kc: e.matmul(b[:, 0:32], lhsT=self.hT[:, kc, ts], rhs=wab[:, kc, :],
                                                  start=(kc == 0), stop=(kc == 7))) for kc in range(8)], [self.hT.k, wab.k], [b.k])
                ta, tb = t16[(tile % 2) * 2], t16[(tile % 2) * 2 + 1]
                cx.op("dve", lambda e: e.tensor_tensor(out=ta[:], in0=b[:, 0:16], in1=dtb[:], op=ALU.add), [b.k, dtb.k], [ta.k])
                cx.op("act", lambda e: e.activation(out=ta[:], in_=ta[:], func=AF.Exp), [ta.k], [ta.k])
                cx.op("act", lambda e: e.activation(out=ta[:], in_=ta[:], func=AF.Ln, bias=1.0), [ta.k], [ta.k])
                cx.op("dve", lambda e: e.tensor_tensor(out=gall[:, tile, :], in0=ta[:], in1=nA[:], op=ALU.mult), [ta.k, nA.k], [gall.k])
                cx.op("act", lambda e: e.activation(out=tb[:], in_=b[:, 16:32], func=AF.Exp, scale=-1.0), [b.k], [tb.k])
                cx.op("dve", lambda e: e.tensor_scalar(out=tb[:], in0=tb[:], scalar1=1.0, scalar2=None, op0=ALU.add), [tb.k], [tb.k])
                cx.op("dve", lambda e: e.reciprocal(out=ball[:, tile, :], in_=tb[:]), [tb.k], [ball.k])
            ss = ExitStack()
            ss.__enter__()
            self.scope = ss
            mk2 = self.sb("p4_mk2", [128, 4, 128])
            for m_ in range(4):
                cx.dma(mk2[:, m_, :], self.i["k_mask"][4 + m_], writes=[mk2.k])
            identb = self.ident[:].unsqueeze(1).to_broadcast([128, 8, 128])
            CH = []
            for r in range(2):
                c_ = {}
                nm = lambda n: "p4_%s_r%d" % (n, r)
                c_["S2"] = self.sb(nm("S2"), [128, 4, 64])
                c_["S2b"] = self.sb(nm("S2b"), [128, 4, 64], BF16)
                c_["bge"] = self.sb(nm("bge"), [128, 8])
                c_["ex"] = self.sb(nm("ex"), [128, 24])
                c_["G2"] = self.sb(nm("G2"), [128, 8, 128])
                c_["Dm"] = self.sb(nm("Dm"), [128, 8, 128], BF16)
                c_["bsm"] = self.sb(nm("bsm"), [128, 8, 128], BF16)
                c_["KbG"] = self.sb(nm("KbG"), [128, 8, 64], BF16)
                c_["Kdec"] = self.sb(nm("Kdec"), [128, 8, 64], BF16)
                c_["Vb"] = self.sb(nm("Vb"), [128, 8, 64], BF16)
                c_["Xf"] = self.sb(nm("Xf"), [128, 8, 128], BF16, quarters=True)
                c_["Yf"] = self.sb(nm("Yf"), [128, 8, 128], BF16, quarters=True)
                c_["AtT"] = self.sb(nm("AtT"), [128, 8, 128], BF16)
                c_["Xb"] = [self.sb(nm("X%d" % b), [128, 8, 128], BF16, quarters=True) for b in range(2)]
                c_["Yb"] = [self.sb(nm("Y%d" % b), [128, 8, 128], BF16, quarters=True) for b in range(2)]
                c_["At"] = c_["Yb"][1]
                c_["Rb"] = [self.sb(nm("R%d" % b), [128, 8, 128], BF16, quarters=True) for b in range(2)]
                c_["Rnb"] = [self.sb(nm("Rn%d" % b), [128, 8, 128], BF16, quarters=True) for b in range(2)]
                c_["Xo2"] = self.sb(nm("Xo2"), [128, 8, 128], BF16, quarters=True)
                c_["Yo"] = [self.sb(nm("Yo%d" % b), [128, 8, 128], BF16, quarters=True) for b in range(2)]
                c_["WTn"] = self.sb(nm("WTn"), [128, 8, 128], BF16)
                c_["vnew"] = self.sb(nm("vnew"), [128, 8, 64], BF16)
                c_["t1"] = self.sb(nm("t1"), [128, 8, 64])
                c_["oft"] = self.sb(nm("oft"), [128, 512], BF16)
                CH.append(c_)

            def bc8(ap, n):
                return ap.unsqueeze(2).to_broadcast([128, 8, n])

            def mask8(m_):
                return mk2[:, m_, :].unsqueeze(1).to_broadcast([128, 8, 128])

            def chain(r):
                c_ = CH[r]
                S2, S2b, bge, ex, G2, Dm, bsm, At = c_["S2"], c_["S2b"], c_["bge"], c_["ex"], c_["G2"], c_["Dm"], c_["bsm"], c_["At"]
                KbG, Kdec, Vb, X, Y, AtT = c_["KbG"], c_["Kdec"], c_["Vb"], c_["Xf"], c_["Yf"], c_["AtT"]
                Xb, Yb, Rb, Rnb, Xo2, Yo = c_["Xb"], c_["Yb"], c_["Rb"], c_["Rnb"], c_["Xo2"], c_["Yo"]
                WTn, vnew, t1, ofb = c_["WTn"], c_["vnew"], c_["t1"], c_["oft"]
                Dsb = G2
                m_incl, m_g2, m_strict, m_dec, m_allow = (3, 2)[r], (0, 1)[r], (0, 1)[r], (0, 1)[r], (2, 3)[r]
                for q in range(nseq):
                    order = list(range(tps)) if r == 0 else list(range(tps - 1, -1, -1))
                    for ci, c in enumerate(order):
                        first, last = (ci == 0), (ci == tps - 1)
                        tile = q * tps + c
                        ts = slice(tile * 128, (tile + 1) * 128)
                        g8 = gall[:, tile, r * 8:(r + 1) * 8]
                        be8 = ball[:, tile, r * 8:(r + 1) * 8]
                        bcu = self.bank("c")
                        cx.mm([lambda e: e.matmul(bcu[:, 0:8], lhsT=mk[:, m_incl, :], rhs=g8, start=True, stop=True),
                               lambda e: e.matmul(bcu[:, 8:16], lhsT=mk[:, m_dec, :], rhs=g8, start=True, stop=True),
                               lambda e: e.matmul(bcu[:, 16:24], lhsT=self.ones_f[:], rhs=g8, start=True, stop=True)],
                              [mk.k, gall.k, self.ones_f.k], [bcu.k])
                        cx.op("act", lambda e: e.activation(out=ex[:], in_=bcu[:, 0:24], func=AF.Exp), [bcu.k], [ex.k])
                        cx.op("dve", lambda e: e.tensor_tensor(out=bge[:], in0=be8, in1=ex[:, 0:8], op=ALU.mult), [ball.k, ex.k], [bge.k])
                        cx.op("pool", lambda e: e.tensor_tensor(out=bsm[:], in0=bc8(be8, 128),
                                                                in1=mk[:, m_strict, :].unsqueeze(1).to_broadcast([128, 8, 128]), op=ALU.mult),
                              [ball.k, mk.k], [bsm.k])
                        cx.op("dve", lambda e: e.tensor_tensor(out=G2[:], in0=bc8(g8, 128),
                                                               in1=mk[:, m_g2, :].unsqueeze(1).to_broadcast([128, 8, 128]), op=ALU.mult),
                              [gall.k, mk.k], [G2.k])
                        yield
                        bk = self.bank("b")
                        bkv = bk.t[:].bitcast(BF16)
                        for hp in range(4):
                            cx.mm([lambda e: e.transpose(out=bkv[:, hp * 128:(hp + 1) * 128], in_=kT[:, hp, ts], identity=self.ident[:])],
                                  [kT.k, self.ident.k], [bk.k])
                        bv_ = self.bank("b")
                        bvv = bv_.t[:].bitcast(BF16)
                        for hp in range(4):
                            cx.mm([lambda e: e.transpose(out=bvv[:, hp * 128:(hp + 1) * 128], in_=vT[:, hp, ts], identity=self.ident[:])],
                                  [vT.k, self.ident.k], [bv_.k])
                        k3 = bkv[:, 0:512].rearrange("p (h d) -> p h d", h=8)
                        v3 = bvv[:, 0:512].rearrange("p (h d) -> p h d", h=8)
                        cx.op("dve", lambda e: e.tensor_tensor(out=KbG[:], in0=k3, in1=bc8(bge[:], 64), op=ALU.mult), [bk.k, bge.k], [KbG.k])
                        cx.op("dve", lambda e: e.tensor_tensor(out=Kdec[:], in0=k3, in1=bc8(ex[:, 8:16], 64), op=ALU.mult), [bk.k, ex.k], [Kdec.k])
                        cx.op("dve", lambda e: e.tensor_tensor(out=Vb[:], in0=v3, in1=bc8(be8, 64), op=ALU.mult), [bv_.k, ball.k], [Vb.k])
                        yield
                        for hb in range(2):
                            bE = self.bank("a")
                            cx.mm([lambda e: e.matmul(bE[:, :], lhsT=mk[:, m_incl, :],
                                                      rhs=G2[:, hb * 4:(hb + 1) * 4, :].rearrange("p h j -> p (h j)"), start=True, stop=True)],
                                  [mk.k, G2.k], [bE.k])
                            cx.op("act", lambda e: e.activation(out=Dm[:, hb * 4:(hb + 1) * 4, :].rearrange("p h j -> p (h j)"),
                                                                in_=bE[:, :], func=AF.Exp), [bE.k], [Dm.k])
                            yield
                        cx.op("dve", lambda e: e.tensor_tensor(out=Dsb[:], in0=Dm[:], in1=bsm[:], op=ALU.mult), [Dm.k, bsm.k], [Dsb.k])
                        cx.op("pool", lambda e: e.tensor_tensor(out=Dm[:], in0=Dm[:],
                                                                in1=mk[:, m_allow, :].unsqueeze(1).to_broadcast([128, 8, 128]), op=ALU.mult),
                              [Dm.k, mk.k], [Dm.k])
                        yield
                        for e_ in range(2):
                            pe = slice(e_ * 64, (e_ + 1) * 64)
                            bKK = self.bank("a")
                            bQK = self.bank("a")
                            fns = []
                            for hp in range(4):
                                fns.append(lambda e, hp=hp: e.matmul(bKK[:, hp * 128:(hp + 1) * 128], lhsT=kT[pe, hp, ts],
                                                                     rhs=kT[pe, hp, ts], start=True, stop=True))
                                fns.append(lambda e, hp=hp: e.matmul(bQK[:, hp * 128:(hp + 1) * 128], lhsT=qT[pe, hp, ts],
                                                                     rhs=kT[pe, hp, ts], start=True, stop=True))
                            cx.mm(fns, [kT.k, qT.k], [bKK.k, bQK.k])
                            cx.op("dve", lambda e: e.tensor_tensor(out=X[:, e_::2, :], in0=bKK[:, :].rearrange("p (h j) -> p h j", h=4),
                                                                   in1=Dsb[:, e_::2, :], op=ALU.mult), [bKK.k, Dsb.k], X.ka)
                            cx.op("dve", lambda e: e.tensor_tensor(out=At[:, e_::2, :], in0=bQK[:, :].rearrange("p (h j) -> p h j", h=4),
                                                                   in1=Dm[:, e_::2, :], op=ALU.mult), [bQK.k, Dm.k], At.ka)
                            yield
                        bY = self.bank("b")
                        bYv = bY.t[:].bitcast(BF16)
                        for h in range(8):
                            cx.mm([lambda e: e.transpose(out=bYv[:, h * 128:(h + 1) * 128], in_=X[:, h, :], identity=self.ident[:])],
                                  X.ka + [self.ident.k], [bY.k])
                        self.copy("act", Y[:].rearrange("p h j -> p (h j)"), bYv[:, :], [bY.k], Y.ka)
                        yield
                        bA = self.bank("b")
                        bAv = bA.t[:].bitcast(BF16)
                        for h in range(8):
                            cx.mm([lambda e: e.transpose(out=bAv[:, h * 128:(h + 1) * 128], in_=At[:, h, :], identity=self.ident[:])],
                                  At.ka + [self.ident.k], [bA.k])
                        self.copy("act", AtT[:].rearrange("p h j -> p (h j)"), bAv[:, :], [bA.k], [AtT.k])
                        yield
                        if first:
                            if kind == "p":
                                cx.op("pool", lambda e: e.memset(S2[:], 0.0), (), [S2.k])
                            else:
                                for e_ in range(2):
                                    src = self.i["sgdn"][l, r].rearrange("(hp e) k v -> e k hp v", e=2)[e_]
                                    cx.dma(S2[e_ * 64:(e_ + 1) * 64, :, :], src, writes=[S2.k])
                            self.copy("act", S2b[:], S2[:], [S2.k], [S2b.k])
                        Xc, Yc = Xb[1], Yb[1]
                        cx.op("dve", lambda e: e.tensor_tensor(out=Xc[:], in0=X[:], in1=mask8(0), op=ALU.mult), X.ka + [mk2.k], Xc.ka)
                        cx.op("dve", lambda e: e.tensor_tensor(out=Yc[:], in0=Y[:], in1=mask8(0), op=ALU.mult), Y.ka + [mk2.k], Yc.ka)
                        for m_ in range(2):
                            cx.op("pool", lambda e: e.tensor_tensor(out=Yo[m_][:], in0=Y[:], in1=mask8(1 + m_), op=ALU.mult), Y.ka + [mk2.k], Yo[m_].ka)
                        cx.op("pool", lambda e: e.tensor_tensor(out=Xo2[:], in0=X[:], in1=mask8(3), op=ALU.mult), X.ka + [mk2.k], Xo2.ka)
                        Rn = Rnb[0]
                        cx.op("dve", lambda e: e.tensor_tensor(out=Rn[:], in0=identb, in1=Xc[:], op=ALU.subtract), [self.ident.k] + Xc.ka, Rn.ka)

                        def mm8(dst, lhs, rhs, src=None, op=None):
                            for hq in range(4):
                                b = self.bank("a")
                                hs = slice(hq * 2, hq * 2 + 2)
                                cx.mm([(lambda e, hh=hh: e.matmul(b[:, hh * 128:(hh + 1) * 128], lhsT=lhs[:, hq * 2 + hh, :], rhs=rhs[:, hq * 2 + hh, :],
                                                                  start=True, stop=True)) for hh in range(2)], [lhs.kq[hq], rhs.kq[hq]], [b.k])
                                dv = dst[:, hs, :].rearrange("p h j -> p (h j)")
                                if src is None:
                                    self.copy("act", dv, b[:, 0:256], [b.k], [dst.kq[hq]])
                                else:
                                    cx.op("dve", lambda e: e.tensor_tensor(out=dv, in0=src[:, hs, :].rearrange("p h j -> p (h j)"), in1=b[:, 0:256], op=op),
                                          [b.k, src.kq[hq]], [dst.kq[hq]])

                        def tr8(dst, src):
                            bt_ = self.bank("b")
                            btv_ = bt_.t[:].bitcast(BF16)
                            for h in range(8):
                                cx.mm([lambda e: e.transpose(out=btv_[:, h * 128:(h + 1) * 128], in_=src[:, h, :], identity=self.ident[:])],
                                      [src.kq[h // 2], self.ident.k], [bt_.k])
                            self.copy("act", dst[:].rearrange("p h j -> p (h j)"), btv_[:, :], [bt_.k], dst.ka)
                        pp = 0
                        for lvl in range(3):
                            Xn, Yn = (Xb[0], Yb[0]) if Xc is Xb[1] else (Xb[1], Yb[1])
                            mm8(Xn, Yc, Xc)
                            yield
                            tr8(Yn, Xn)
                            yield
                            Rn2 = Rnb[(pp + 1) % 2]
                            mm8(Rn2, Yn, Rn, src=Rn, op=ALU.add)
                            yield
                            Xc, Yc, Rn = Xn, Yn, Rn2
                            pp += 1
                        Rt = Rb[0]
                        tr8(Rt, Rn)
                        yield
                        P1 = Xb[0] if Xc is Xb[1] else Xb[1]
                        for m_ in range(2):
                            Rn2 = Rnb[(pp + 1) % 2]
                            mm8(P1, Yo[m_], Rn)
                            yield
                            mm8(Rn2, Rt, P1, src=Rn, op=ALU.subtract)
                            yield
                            Rt2 = Rb[(m_ + 1) % 2]
                            tr8(Rt2, Rn2)
                            yield
                            Rn, Rt = Rn2, Rt2
                            pp += 1
                        Rt2 = Rb[1] if Rt is Rb[0] else Rb[0]
                        mm8(P1, Xo2, Rt)
                        yield
                        mm8(Rt2, Rn, P1, src=Rt, op=ALU.subtract)
                        yield
                        R = Rt2
                        for hb in range(2):
                            b = self.bank("a")
                            fns = []
                            for hh in range(4):
                                h = hb * 4 + hh
                                pe = slice((h % 2) * 64, (h % 2) * 64 + 64)
                                fns.append(lambda e, hh=hh, h=h, pe=pe: e.matmul(b[pe, hh * 128:(hh + 1) * 128], lhsT=KbG[:, h, :], rhs=R[:, h, :],
                                                                                 start=True, stop=True))
                            cx.mm(fns, [KbG.k] + R.ka, [b.k])
                            bv4 = b[:, :].rearrange("p (h n) -> p h n", h=4)
                            for e_ in range(2):
                                pe = slice(e_ * 64, (e_ + 1) * 64)
                                cx.op("act", lambda e: e.activation(out=WTn[pe, hb * 4 + e_:hb * 4 + 4:2, :], in_=bv4[pe, e_::2, :], func=AF.Copy, scale=-1.0),
                                      [b.k], [WTn.k])
                        yield
                        dd = (l == 0 and kind == self.dbg.get("kind", "p") and tile == self.dbg.get("tile", 0) and r == self.dbg.get("r", 0))
                        self.ddump("R", R[:].rearrange("p h j -> p (h j)"), R.ka, dd)
                        bvn = self.bank("c")
                        fns = []
                        for h in range(8):
                            hp, e_ = divmod(h, 2)
                            pe = slice(e_ * 64, (e_ + 1) * 64)
                            fns.append(lambda e, h=h: e.matmul(bvn[:, h * 64:(h + 1) * 64], lhsT=R[:, h, :], rhs=Vb[:, h, :], start=True, stop=False))
                            fns.append(lambda e, h=h, hp=hp, pe=pe: e.matmul(bvn[:, h * 64:(h + 1) * 64], lhsT=WTn[pe, h, :], rhs=S2b[pe, hp, :],
                                                                             start=False, stop=True))
                        cx.mm(fns, R.ka + [Vb.k, WTn.k, S2b.k], [bvn.k])
                        self.copy("act", vnew[:].rearrange("p h d -> p (h d)"), bvn[:, :], [bvn.k], [vnew.k])
                        yield
                        bo1 = self.bank("c")
                        bo2 = self.bank("c")
                        fns = []
                        for h in range(8):
                            hp, e_ = divmod(h, 2)
                            pe = slice(e_ * 64, (e_ + 1) * 64)
                            fns.append(lambda e, h=h, hp=hp, pe=pe: e.matmul(bo1[:, h * 64:(h + 1) * 64], lhsT=qT[pe, hp, ts], rhs=S2b[pe, hp, :],
                                                                             start=True, stop=True))
                            fns.append(lambda e, h=h: e.matmul(bo2[:, h * 64:(h + 1) * 64], lhsT=AtT[:, h, :], rhs=vnew[:, h, :], start=True, stop=True))
                        cx.mm(fns, [qT.k, S2b.k, AtT.k, vnew.k], [bo1.k, bo2.k])
                        cx.op("dve", lambda e: e.tensor_tensor(out=t1[:], in0=bo1[:, :].rearrange("p (h d) -> p h d", h=8), in1=bc8(ex[:, 0:8], 64),
                                                               op=ALU.mult), [bo1.k, ex.k], [t1.k])
                        cx.op("dve", lambda e: e.tensor_tensor(out=ofb[:], in0=bo2[:, :], in1=t1[:].rearrange("p h d -> p (h d)"), op=ALU.add),
                              [bo2.k, t1.k], [ofb.k])
                        kk = Tok("ofs")
                        self.k_ofs[(r, t0, tile)] = kk
                        cx.dma(self.ofs[r, t0 + tile * 128:t0 + (tile + 1) * 128, :], ofb[:], reads=[ofb.k], writes=[kk], q=self.store_q)
                        bs = self.bank("c")
                        fns = []
                        for h in range(8):
                            hp, e_ = divmod(h, 2)
                            pe = slice(e_ * 64, (e_ + 1) * 64)
                            fns.append(lambda e, h=h, hp=hp, pe=pe: e.matmul(bs[pe, hp * 64:(hp + 1) * 64], lhsT=Kdec[:, h, :], rhs=vnew[:, h, :],
                                                                             start=True, stop=True))
                        cx.mm(fns, [Kdec.k, vnew.k], [bs.k])
                        for e_ in range(2):
                            pe = slice(e_ * 64, (e_ + 1) * 64)
                            eg = ex[pe, 16:24].rearrange("p (hp e) -> p hp e", e=2)[:, :, e_].unsqueeze(2).to_broadcast([64, 4, 64])
                            cx.op("dve", lambda e: e.tensor_tensor(out=S2[pe, :, :], in0=S2[pe, :, :], in1=eg, op=ALU.mult), [S2.k, ex.k], [S2.k])
                            cx.op("dve", lambda e: e.tensor_tensor(out=S2[pe, :, :], in0=S2[pe, :, :],
                                                                   in1=bs[pe, 0:256].rearrange("p (hp d) -> p hp d", hp=4), op=ALU.add), [S2.k, bs.k], [S2.k])
                        self.copy("act", S2b[:], S2[:], [S2.k], [S2b.k])
                        if last and kind == "p":
                            for e_ in range(2):
                                dst = self.o["nst"][q, l, r].rearrange("(hp e) k v -> e k hp v", e=2)[e_]
                                self.out_dma(dst, S2[e_ * 64:(e_ + 1) * 64, :, :], [S2.k])
                        yield

            gens = [chain(0), chain(1)]
            for _ in range(14):
                next(gens[0], None)
            active = list(gens)
            while active:
                for g in list(active):
                    try:
                        next(g)
                    except StopIteration:
                        active.remove(g)
            scan_bufs = [mk2]
            for c_ in CH:
                for k_, v_ in c_.items():
                    if k_ == "At":
                        continue
                    scan_bufs += v_ if isinstance(v_, list) else [v_]
            self.end_scope(scan_bufs)
            ss.__exit__(None, None, None)
            self.scope = sc
            wz = self.sb("p4_wz", [128, 8, 512], BF16)
            self.load_w(self.i["w_in"][l][:, C_Z:C_Z + 512], wz, wz.k, D, 512)
            ofl = [self.sb("p4_ofl%d" % b, [128, 2, 512], BF16) for b in range(2)]
            osum = self.sb("p4_osum", [128, 8, 64])
            sqo = self.sb("p4_sqo", [128, 8, 64])
            st8 = self.sb("p4_st8", [128, 24])
            szt = self.sb("p4_sz", [128, 512])
            ocb = self.sb("p4_oc", [128, 512], BF16)
            ocT = [self.sb("p4_ocT%d" % b, [128, 4, 512], BF16) for b in range(2)]
            for tile in range(ntile):
                ts = slice(tile * 128, (tile + 1) * 128)
                ofb = ofl[tile % 2]
                for r in range(2):
                    cx.dma(ofb[:, r, :], self.ofs[r, t0 + tile * 128:t0 + (tile + 1) * 128, :], reads=[self.k_ofs[(r, t0, tile)]],
                           writes=[ofb.k])
                cx.op("pool", lambda e: e.tensor_tensor(out=osum[:].rearrange("p h d -> p (h d)"), in0=ofb[:, 0, :], in1=ofb[:, 1, :], op=ALU.add),
                      [ofb.k], [osum.k])
                cx.op("act", lambda e: e.activation(out=sqo[:], in_=osum[:], func=AF.Square), [osum.k], [sqo.k])
                cx.op("dve", lambda e: e.tensor_reduce(out=st8[:, 0:8], in_=sqo[:], axis=AX.X, op=ALU.add), [sqo.k], [st8.k])
                self.rsqrt_mean(st8[:, 0:8], st8[:, 16:24], 64, [st8.k], [st8.k], st8[:, 8:16])
                cx.op("dve", lambda e: e.tensor_tensor(out=osum[:], in0=osum[:], in1=bc8(st8[:, 16:24], 64), op=ALU.mult),
                      [osum.k, st8.k], [osum.k])
                cx.op("pool", lambda e: e.tensor_tensor(out=osum[:], in0=osum[:], in1=gn[:].unsqueeze(1).to_broadcast([128, 8, 64]),
                                                        op=ALU.mult), [osum.k, gn.k], [osum.k])
                bz = self.bank("a")
                cx.mm([(lambda e, kc=kc: e.matmul(bz[:, :], lhsT=self.hT[:, kc, ts], rhs=wz[:, kc, :], start=(kc == 0), stop=(kc == 7)))
                       for kc in range(8)], [self.hT.k, wz.k], [bz.k])
                cx.op("act", lambda e: e.activation(out=szt[:], in_=bz[:, :], func=AF.Silu), [bz.k], [szt.k])
                cx.op("dve", lambda e: e.tensor_tensor(out=ocb[:], in0=osum[:].rearrange("p h d -> p (h d)"), in1=szt[:], op=ALU.mult),
                      [osum.k, szt.k], [ocb.k])
                bt = self.bank("b")
                btv = bt.t[:].bitcast(BF16)
                for j in range(4):
                    cx.mm([lambda e: e.transpose(out=btv[:, j * 128:(j + 1) * 128], in_=ocb[:, j * 128:(j + 1) * 128],
                                                 identity=self.ident[:])], [ocb.k, self.ident.k], [bt.k])
                blk = tile // 4
                oc_ = ocT[blk % 2]
                self.copy("act", oc_[:, :, (tile % 4) * 128:(tile % 4 + 1) * 128], btv[:, 0:512].rearrange("p (j n) -> p j n", j=4),
                          [bt.k], [oc_.k])
                if tile % 4 == 3 or tile == ntile - 1:
                    a0 = blk * 512
                    n_ = (tile + 1) * 128 - a0
                    for j in range(4):
                        self.store_oT(8 + j, t0, a0, n_, oc_[:, j, 0:n_], [oc_.k])
            bufs = [qT, kT, vT, gall, ball, wab, mk, bd1, nA, dtb, gn, wz, osum, sqo, st8, szt, ocb] + t16 + ofl + ocT
            self.end_scope(bufs)
            self.scope = old

    def phase5(self, l, t0, T, cond):
        cx = self.cx
        with ExitStack() as sc:
            old = self.scope
            self.scope = sc
            mT = self.sb("mT", [128, 8, T], BF16)
            wmg2 = [self.sb("p5_wmg%d" % b, [128, 8, 3, 128], BF16) for b in range(2)]
            wp2 = [self.sb("p5_wp%d" % b, [128, 3, 4, 128], BF16) for b in range(2)]
            sg = [self.sb("p5_sg%d" % b, [128, 512]) for b in range(3)]
            tm = [self.sb("p5_tm%d" % b, [128, 512]) for b in range(3)]
            wpsrc = [self.i["w_pa"], self.i["w_pb"], self.i["w_pc"]]
            nblk = (T + 511) // 512
            oTb = self.sb("p5_oT", [128, 12, T], BF16)
            for jj in range(12):
                cx.dma(oTb[:, jj, :], self.oTs[jj, :, t0:t0 + T], reads=self.oT_toks(jj, t0, T), writes=[oTb.k])
            for j in range(8):
                wmg, wp = wmg2[j % 2], wp2[j % 2]
                for br in range(3):
                    self.load_w(self.i["w_in"][l][:, C_MG + br * D + j * 128:C_MG + br * D + (j + 1) * 128],
                                wmg[:, :, br, :], wmg.k, D, 128)
                    self.load_w(wpsrc[br][l][:, j * 128:(j + 1) * 128], wp[:, br, :, :], wp.k, 512, 128)
                for blk in range(nblk):
                    n = min(512, T - blk * 512)
                    ts = slice(blk * 512, blk * 512 + n)
                    for br in range(3):
                        bg = self.bank()
                        cx.mm([(lambda e, kc=kc: e.matmul(bg[:, 0:n], lhsT=wmg[:, kc, br, :], rhs=self.hT[:, kc, ts],
                                                          start=(kc == 0), stop=(kc == 7))) for kc in range(8)],
                              [wmg.k, self.hT.k], [bg.k])
                        cx.op("act", lambda e: e.activation(out=sg[br][:, 0:n], in_=bg[:, 0:n], func=AF.Sigmoid),
                              [bg.k], [sg[br].k])
                        bp = self.bank()
                        cx.mm([(lambda e, kc=kc: e.matmul(bp[:, 0:n], lhsT=wp[:, br, kc, :],
                                                          rhs=oTb[:, br * 4 + kc, ts],
                                                          start=(kc == 0), stop=(kc == 3))) for kc in range(4)],
                              [wp.k, oTb.k], [bp.k])
                        cx.op("dve", lambda e: e.tensor_tensor(out=tm[br][:, 0:n], in0=bp[:, 0:n], in1=sg[br][:, 0:n],
                                                               op=ALU.mult), [bp.k, sg[br].k], [tm[br].k])
                    cx.op("pool", lambda e: e.tensor_tensor(out=tm[0][:, 0:n], in0=tm[0][:, 0:n], in1=tm[1][:, 0:n],
                                                            op=ALU.add), [tm[0].k, tm[1].k], [tm[0].k])
                    cx.op("pool", lambda e: e.tensor_tensor(out=mT[:, j, ts], in0=tm[0][:, 0:n], in1=tm[2][:, 0:n],
                                                            op=ALU.add), [tm[0].k, tm[2].k], [mT.k])
            wo = self.sb("p5_wo", [128, 8, D], BF16)
            self.load_w(self.i["w_o"][l], wo, wo.k, D, D)
            xin = [self.sb("p5_x%d" % b, [128, D]) for b in range(2)]
            xo = [self.sb("p5_xo%d" % b, [128, D]) for b in range(2)]
            junk = self.sb("p5_junk", [128, D], BF16)
            st = self.sb("p5_st", [128, 4])
            last = (l == DEPTH - 1)
            self.fng_bc = self.sb("fng_bc", [128, D])
            if last:
                cx.dma(self.fng_bc[:], self.i["final_norm_g"].partition_broadcast(128), writes=[self.fng_bc.k])
            for tile in range(T // 128):
                xb, xob = xin[tile % 2], xo[tile % 2]
                src, srck = self.x_src(l, t0, tile)
                cx.dma(xb[:], src, reads=[srck] if srck else [], writes=[xb.k])
                for hf in range(2):
                    b = self.bank()
                    cx.mm([(lambda e, kc=kc: e.matmul(b[:, :], lhsT=mT[:, kc, tile * 128:(tile + 1) * 128],
                                                      rhs=wo[:, kc, hf * 512:(hf + 1) * 512],
                                                      start=(kc == 0), stop=(kc == 7))) for kc in range(8)],
                          [mT.k, wo.k], [b.k])
                    cs = slice(hf * 512, (hf + 1) * 512)
                    cx.op("dve", lambda e: e.tensor_tensor(out=xob[:, cs], in0=b[:, :], in1=self.gate_bc[:, cond, cs],
                                                           op=ALU.mult), [b.k, self.gate_bc.k], [xob.k])
                    cx.op("pool", lambda e: e.tensor_tensor(out=xob[:, cs], in0=xob[:, cs], in1=xb[:, cs], op=ALU.add),
                          [xob.k, xb.k], [xob.k])
                r0 = t0 + tile * 128
                if not last:
                    cx.dma(self.xres[r0:r0 + 128, :], xob[:], reads=[xob.k], writes=[self.k_xres[r0]], q=self.store_q)
                else:
                    cx.op("pool", lambda e: e.memset(st[:, 0:1], 0.0), (), [st.k])
                    cx.op("act", lambda e: e.activation(out=junk[:], in_=xob[:], func=AF.Square, accum_out=st[:, 0:1]),
                          [xob.k], [junk.k, st.k])
                    self.rsqrt_mean(st[:, 0:1], st[:, 2:3], D, [st.k], [st.k], st[:, 1:2])
                    cx.op("dve", lambda e: e.scalar_tensor_tensor(out=xb[:], in0=xob[:], scalar=st[:, 2:3],
                                                                  in1=self.fng_bc[:], op0=ALU.mult, op1=ALU.mult),
                          [xob.k, st.k, self.fng_bc.k], [xb.k])
                    if r0 < self.TP:
                        dst = self.o["yp"][r0:r0 + 128, :]
                    else:
                        dst = self.o["ys"][r0 - self.TP:r0 - self.TP + 128, :]
                    self.out_dma(dst, xb[:], [xb.k])
            self.end_scope([mT, wo, junk, st, oTb, self.fng_bc] + wmg2 + wp2 + sg + tm + xin + xo)
            self.scope = old

    def ddump(self, name, ap, toks, cond=True):
        key = "d_" + name
        if key not in self.o or not cond or key in self._dumped:
            return
        self._dumped.add(key)
        shp = list(ap.shape)
        tmp = self.sb("dd_" + name, shp)
        self.copy("dve", tmp[:], ap, toks, [tmp.k])
        self.cx.dma(self.o[key][0:shp[0]], tmp[:], reads=[tmp.k])

    def out_dma(self, dst, src, reads):
        k = Tok("out")
        self.out_toks.append(k)
        self.cx.dma(dst, src, reads=reads, writes=[k], q=self.store_q)


def host_consts(SS):
    ident = np.eye(128, dtype=np.float32)
    t = np.arange(SS)
    row = (t // 64).astype(np.float32)
    col = (t % 64).astype(np.float32)
    inv = (10000.0 ** (-np.arange(8, dtype=np.float32) / 8)).astype(np.float32)
    ang = np.concatenate([row[:, None] * inv, col[:, None] * inv], axis=-1).astype(np.float32)
    cos, sin = np.cos(ang), np.sin(ang)
    cosE = np.ones((96, SS), np.float32)
    sinE = np.zeros((96, SS), np.float32)
    for dd in range(32):
        cosE[64 + dd] = cos[:, dd // 2]
        sinE[64 + dd] = sin[:, dd // 2]
    k = np.arange(128)
    masks = np.zeros((8, 128, 128), np.float32)
    masks[0] = (k[:, None] > k[None, :])
    masks[1] = (k[:, None] < k[None, :])
    masks[2] = (k[:, None] >= k[None, :])
    masks[3] = (k[:, None] <= k[None, :])
    same = lambda n: (k[:, None] // n == k[None, :] // n)
    masks[4] = same(16)
    masks[5] = same(32) & ~same(16)
    masks[6] = same(64) & ~same(32)
    masks[7] = ~same(64)
    return {"k_ident": ident, "k_rope": np.stack([cosE, sinE]).astype(np.float32), "k_mask": masks}


def make_in_maps(inputs, n_cores, NP, SP, SS, PAST):
    f = lambda a: np.ascontiguousarray(np.asarray(a, dtype=np.float32))
    consts = host_consts(SS)
    maps = []
    for i in range(n_cores):
        m = {}
        m["xp"] = f(inputs["x_prompt"][i * NP:(i + 1) * NP]).reshape(NP * SP, D)
        m["xs"] = f(inputs["x_sample"][i]).reshape(SS, D)
        m["cvec"] = f(np.stack([inputs["c_ctx"], inputs["c"][i]]))
        m["cckv"] = f(inputs["cache_ckv"][i])
        m["ckpe"] = f(inputs["cache_kpe"][i])
        m["sgdn"] = f(inputs["state_gdn"][i])
        for k in ("norm_g", "w_ada", "b_ada", "w_in", "q_norm_g", "kv_norm_g", "w_uq", "w_ukv", "conv_b_w",
                  "conv_qkv_w", "gdn_norm_g", "w_pa", "w_pb", "w_pc", "w_o", "final_norm_g"):
            m[k] = f(inputs[k])
        m["a_log"] = f(inputs["a_log"]).reshape(DEPTH, 16)
        m["dt_bias"] = f(inputs["dt_bias"]).reshape(DEPTH, 16)
        m.update(consts)
        maps.append(m)
    return maps


_NC_CACHE = {}


def kernel(**inputs):
    n = 8
    NP, SP, SS, PAST = 4, 256, 2048, 256
    key = (NP, SP, SS, PAST)
    if key not in _NC_CACHE:
        _NC_CACHE[key] = Builder(NP, SP, SS, PAST).build()
    nc = _NC_CACHE[key]
    maps = make_in_maps(inputs, n, NP, SP, SS, PAST)
    res = run_bass_kernel_spmd(nc, maps, core_ids=list(range(n)))
    r = res.results
    yp = np.concatenate([r[i]["yp"].reshape(NP, SP, D) for i in range(n)], axis=0)
    ys = np.stack([r[i]["ys"].reshape(SS, D) for i in range(n)], axis=0)
    nckv = np.concatenate([r[i]["nckv"] for i in range(n)], axis=0)
    nkpe = np.concatenate([r[i]["nkpe"] for i in range(n)], axis=0)
    nst = np.concatenate([r[i]["nst"] for i in range(n)], axis=0)
    return (yp.astype(np.float32), ys.astype(np.float32), nckv.astype(np.float32), nkpe.astype(np.float32),
            nst.astype(np.float32))
```
